# Optimizing a Trainium2 kernel written in Bass

```python
import math
import jax, jax.numpy as jnp
from jax import lax
import numpy as np

D_MODEL = 1024
BATCH = 16
SEQ = 2048
DEPTH = 1

DN_HEAD_DIM = 128
DN_WIDTH = D_MODEL // 2
DN_HEADS = DN_WIDTH // DN_HEAD_DIM
SHORT_CONV = 4
GLA_VAL_DIM = 128
GLA_WIDTH = D_MODEL - DN_WIDTH
GLA_HEADS = GLA_WIDTH // GLA_VAL_DIM
GLA_KEY_DIM = GLA_VAL_DIM // 2
GLA_GATE_RANK = 16
GLA_TAU = 16.0
MIX_WIDTH = DN_WIDTH + GLA_WIDTH
IN_SPLITS = (DN_WIDTH, DN_WIDTH, DN_WIDTH, DN_WIDTH, DN_HEADS, DN_HEADS,
             GLA_HEADS * GLA_KEY_DIM, GLA_HEADS * GLA_KEY_DIM, GLA_WIDTH, GLA_WIDTH, GLA_GATE_RANK)
IN_WIDTH = sum(IN_SPLITS)
CHUNK = 64
D_FF = 2816
FFN_CONV = 3
ALPHA = (2.0 * DEPTH) ** 0.25
BETA_INIT = (8.0 * DEPTH) ** -0.25
EPS = 1e-6

kernel_name = "hybrid_deltanet_gla_convffn_deepnorm_adaln"


def layer_norm(x, g, b):
    xf = x.astype(jnp.float32)
    mu = xf.mean(-1, keepdims=True)
    var = jnp.square(xf - mu).mean(-1, keepdims=True)
    return ((xf - mu) * lax.rsqrt(var + EPS) * g.astype(jnp.float32) + b.astype(jnp.float32)).astype(x.dtype)


def rms_norm(x, g):
    xf = x.astype(jnp.float32)
    return (xf * lax.rsqrt(jnp.mean(xf * xf, -1, keepdims=True) + EPS) * g.astype(jnp.float32)).astype(x.dtype)


def l2_norm(x):
    return x * lax.rsqrt(jnp.sum(x * x, -1, keepdims=True) + EPS)


def causal_dwconv(x, w):
    k_w, ch = w.shape
    return lax.conv_general_dilated(x, w[:, None, :].astype(x.dtype), window_strides=(1,),
                                    padding=[(k_w - 1, 0)], dimension_numbers=('NWC', 'WIO', 'NWC'),
                                    feature_group_count=ch)


def to_chunks(x):
    b, t, h, d = x.shape
    return x.reshape(b, t // CHUNK, CHUNK, h, d).transpose(1, 0, 3, 2, 4)


def from_chunks(x):
    n, b, h, c, d = x.shape
    return x.transpose(1, 0, 3, 2, 4).reshape(b, n * c, h, d)


def gated_delta_rule(q, k, v, log_a, beta):
    dt = v.dtype
    q, k, v, log_a, beta = (a.astype(jnp.float32) for a in (q, k, v, log_a, beta))
    bsz, _, nh, dk = q.shape
    dv = v.shape[-1]
    qc, kc, vc = to_chunks(q * dk ** -0.5), to_chunks(k), to_chunks(v)
    G = jnp.cumsum(to_chunks(log_a[..., None])[..., 0], axis=-1)
    bc = to_chunks(beta[..., None])[..., 0]
    causal = jnp.tril(jnp.ones((CHUNK, CHUNK), dtype=bool))
    strict = jnp.tril(jnp.ones((CHUNK, CHUNK), dtype=bool), -1)
    decay = jnp.exp(jnp.where(causal, G[..., :, None] - G[..., None, :], -jnp.inf))
    kb = kc * bc[..., None]
    m_low = jnp.where(strict, jnp.einsum('nbhid,nbhjd->nbhij', kb, kc) * decay, 0.0)
    lhs = m_low + jnp.eye(CHUNK, dtype=jnp.float32)
    rhs = jnp.concatenate([vc * bc[..., None], kb * jnp.exp(G)[..., None]], axis=-1)
    sol = lax.linalg.triangular_solve(lhs, rhs, left_side=True, lower=True, unit_diagonal=True)
    u, w = sol[..., :dv], sol[..., dv:]
    attn = jnp.einsum('nbhid,nbhjd->nbhij', qc, kc) * decay
    q_dec = qc * jnp.exp(G)[..., None]
    k_dec = kc * jnp.exp(G[..., -1:] - G)[..., None]
    g_last = jnp.exp(G[..., -1])

    def step(S, inp):
        u_i, w_i, attn_i, qd_i, kd_i, gl_i = inp
        v_new = u_i - jnp.einsum('bhcd,bhde->bhce', w_i, S)
        o = jnp.einsum('bhcd,bhde->bhce', qd_i, S) + jnp.einsum('bhij,bhje->bhie', attn_i, v_new)
        S = S * gl_i[..., None, None] + jnp.einsum('bhcd,bhce->bhde', kd_i, v_new)
        return S, o

    S0 = jnp.zeros((bsz, nh, dk, dv), jnp.float32)
    _, o = lax.scan(step, S0, (u, w, attn, q_dec, k_dec, g_last))
    return from_chunks(o).astype(dt)


def gla_attention(q, k, v, log_alpha):
    dt = v.dtype
    q, k, v, log_alpha = (a.astype(jnp.float32) for a in (q, k, v, log_alpha))
    bsz, _, nh, dk = q.shape
    dv = v.shape[-1]
    qc, kc, vc = to_chunks(q * dk ** -0.5), to_chunks(k), to_chunks(v)
    b = jnp.cumsum(to_chunks(log_alpha), axis=-2)
    causal = jnp.tril(jnp.ones((CHUNK, CHUNK), dtype=bool))
    q_dec = qc * jnp.exp(b)
    attn = jnp.where(causal, jnp.einsum('nbhid,nbhjd->nbhij', q_dec, kc * jnp.exp(-b)), 0.0)
    o_intra = jnp.einsum('nbhij,nbhje->nbhie', attn, vc)
    k_dec = kc * jnp.exp(b[..., -1:, :] - b)
    g_last = jnp.exp(b[..., -1, :])

    def step(S, inp):
        qd_i, kd_i, v_i, oi_i, gl_i = inp
        o = jnp.einsum('bhcd,bhde->bhce', qd_i, S) + oi_i
        S = S * gl_i[..., :, None] + jnp.einsum('bhcd,bhce->bhde', kd_i, v_i)
        return S, o

    S0 = jnp.zeros((bsz, nh, dk, dv), jnp.float32)
    _, o = lax.scan(step, S0, (q_dec, k_dec, vc, o_intra, g_last))
    return from_chunks(o).astype(dt)


def hybrid_mixer(h, w_in, dn_conv, dn_a_log, dn_dt_bias, dn_norm_g, gla_w_gate2, gla_b_gate, gla_norm_g, w_o):
    bsz, t, _ = h.shape
    proj = h @ w_in
    offsets, acc = [], 0
    for s in IN_SPLITS[:-1]:
        acc += s
        offsets.append(acc)
    (dn_q, dn_k, dn_v, dn_z, dn_a, dn_b, gl_q, gl_k, gl_v, gl_g, gl_r) = jnp.split(proj, offsets, axis=-1)
    qkv = jax.nn.silu(causal_dwconv(jnp.concatenate([dn_q, dn_k, dn_v], -1), dn_conv))
    q, k, v = (a.reshape(bsz, t, DN_HEADS, DN_HEAD_DIM) for a in jnp.split(qkv, 3, axis=-1))
    log_a = -jnp.exp(dn_a_log) * jax.nn.softplus(dn_a + dn_dt_bias)
    beta = jax.nn.sigmoid(dn_b)
    o_dn = gated_delta_rule(l2_norm(q), l2_norm(k), v, log_a, beta)
    o_dn = rms_norm(o_dn, dn_norm_g) * jax.nn.silu(dn_z.reshape(bsz, t, DN_HEADS, DN_HEAD_DIM))
    log_alpha = jax.nn.log_sigmoid(gl_r @ gla_w_gate2 + gla_b_gate) / GLA_TAU
    o_gla = gla_attention(gl_q.reshape(bsz, t, GLA_HEADS, GLA_KEY_DIM),
                          gl_k.reshape(bsz, t, GLA_HEADS, GLA_KEY_DIM),
                          gl_v.reshape(bsz, t, GLA_HEADS, GLA_VAL_DIM),
                          log_alpha.reshape(bsz, t, GLA_HEADS, GLA_KEY_DIM))
    o_gla = rms_norm(o_gla, gla_norm_g) * jax.nn.silu(gl_g.reshape(bsz, t, GLA_HEADS, GLA_VAL_DIM))
    o = jnp.concatenate([o_dn.reshape(bsz, t, DN_WIDTH), o_gla.reshape(bsz, t, GLA_WIDTH)], axis=-1)
    return o @ w_o


def conv_ffn(h, w_up, conv_w, conv_b, w_down):
    u = causal_dwconv(h @ w_up, conv_w) + conv_b
    gate, val = jnp.split(u, 2, axis=-1)
    return (jax.nn.silu(gate) * val) @ w_down


def setup_inputs(seed: int = 0) -> dict:
    key = jax.random.key(seed)
    ks = jax.random.split(key, 24)
    nrm = lambda k, shape, s: jax.random.normal(k, shape, jnp.float32) * s
    L, D = DEPTH, D_MODEL
    dt = jnp.exp(jax.random.uniform(ks[9], (L, DN_HEADS), jnp.float32, math.log(1e-3), math.log(1e-1)))
    return {
        "x": nrm(ks[0], (BATCH, SEQ, D), 1.0),
        "c": nrm(ks[1], (BATCH, D), 1.0),
        "ln0_g": 1.0 + nrm(ks[2], (D,), 0.02),
        "ln0_b": nrm(ks[3], (D,), 0.02),
        "w_ada": nrm(ks[4], (L, D, 6 * D), 0.1 * D ** -0.5),
        "b_ada": nrm(ks[5], (L, 6 * D), 0.01),
        "w_in": nrm(ks[6], (L, D, IN_WIDTH), D ** -0.5),
        "dn_conv": nrm(ks[7], (L, SHORT_CONV, 3 * DN_WIDTH), SHORT_CONV ** -0.5),
        "dn_a_log": jnp.log(jax.random.uniform(ks[8], (L, DN_HEADS), jnp.float32, 1.0, 16.0)),
        "dn_dt_bias": dt + jnp.log(-jnp.expm1(-dt)),
        "dn_norm_g": 1.0 + nrm(ks[10], (L, DN_HEAD_DIM), 0.02),
        "gla_w_gate2": nrm(ks[11], (L, GLA_GATE_RANK, GLA_HEADS * GLA_KEY_DIM), GLA_GATE_RANK ** -0.5),
        "gla_b_gate": nrm(ks[12], (L, GLA_HEADS * GLA_KEY_DIM), 0.01),
        "gla_norm_g": 1.0 + nrm(ks[13], (L, GLA_VAL_DIM), 0.02),
        "w_o": nrm(ks[14], (L, MIX_WIDTH, D), MIX_WIDTH ** -0.5 * BETA_INIT),
        "ln1_g": 1.0 + nrm(ks[15], (L, D), 0.02),
        "ln1_b": nrm(ks[16], (L, D), 0.02),
        "ffn_w_up": nrm(ks[17], (L, D, 2 * D_FF), D ** -0.5),
        "ffn_conv": nrm(ks[18], (L, FFN_CONV, 2 * D_FF), FFN_CONV ** -0.5),
        "ffn_conv_b": nrm(ks[19], (L, 2 * D_FF), 0.02),
        "ffn_w_down": nrm(ks[20], (L, D_FF, D), D_FF ** -0.5 * BETA_INIT),
        "ln2_g": 1.0 + nrm(ks[21], (L, D), 0.02),
        "ln2_b": nrm(ks[22], (L, D), 0.02),
    }


def reference(x, c, ln0_g, ln0_b, w_ada, b_ada, w_in, dn_conv, dn_a_log, dn_dt_bias, dn_norm_g,
              gla_w_gate2, gla_b_gate, gla_norm_g, w_o, ln1_g, ln1_b, ffn_w_up, ffn_conv, ffn_conv_b,
              ffn_w_down, ln2_g, ln2_b):
    x = layer_norm(x, ln0_g, ln0_b)
    cond = jax.nn.silu(c)
    for l in range(DEPTH):
        mod = cond @ w_ada[l] + b_ada[l]
        sh_a, sc_a, gt_a, sh_f, sc_f, gt_f = jnp.split(mod[:, None, :], 6, axis=-1)
        h = x * (1.0 + sc_a) + sh_a
        y = hybrid_mixer(h, w_in[l], dn_conv[l], dn_a_log[l], dn_dt_bias[l], dn_norm_g[l],
                         gla_w_gate2[l], gla_b_gate[l], gla_norm_g[l], w_o[l])
        x = layer_norm(ALPHA * x + (1.0 + gt_a) * y, ln1_g[l], ln1_b[l])
        h = x * (1.0 + sc_f) + sh_f
        y = conv_ffn(h, ffn_w_up[l], ffn_conv[l], ffn_conv_b[l], ffn_w_down[l])
        x = layer_norm(ALPHA * x + (1.0 + gt_f) * y, ln2_g[l], ln2_b[l])
    return x
```

```python
import contextlib
import numpy as np
import concourse.bass as bass
import concourse.mybir as mybir
from concourse.bass_utils import run_bass_kernel_spmd

F32 = mybir.dt.float32
BF16 = mybir.dt.bfloat16
AF = mybir.ActivationFunctionType
ALU = mybir.AluOpType

D = 1024
ALPHA = 2.0 ** 0.25
EPS = 1e-6
NP_COLS = 320
O_G0, O_B0, O_G1, O_B1, O_BADA, O_DNC, O_DNG, O_GLG, O_FC, O_FB, O_NAL, O_DTB = 0, 8, 16, 24, 32, 80, 128, 129, 130, 262, 306, 310
WI_GROUPS = [(0, 512), (512, 512), (1024, 512), (1536, 512), (2048, 520), (2568, 512), (3080, 528)]
GW = 528
SAFE_SAME_ENGINE = True


class Buf:
    __slots__ = ("name", "w", "r", "dsem", "dcnt")

    def __init__(self, name):
        self.name = name
        self.w = None
        self.r = {}
        self.dsem = None
        self.dcnt = 0


class Tl:
    def __init__(self, t, name):
        self.t = t
        self.b = Buf(name)

    def __getitem__(self, k):
        return self.t[k]


class Eng:
    def __init__(self, h, sem, name):
        self.h = h
        self.sem = sem
        self.cnt = 0
        self.known = {}
        self.name = name


class FW:
    def __init__(self, nc, es):
        self.nc = nc
        self.es = es
        self.nsem = 0
        self.PE = Eng(nc.tensor, self.newsem("pe"), "pe")
        self.ACT = Eng(nc.scalar, self.newsem("act"), "act")
        self.DVE = Eng(nc.vector, self.newsem("dve"), "dve")
        self.POOL = Eng(nc.gpsimd, self.newsem("pool"), "pool")
        self.SP = Eng(nc.sync, self.newsem("sp"), "sp")
        self.engs = [self.PE, self.ACT, self.DVE, self.POOL, self.SP]
        self.dsems = []

    def newsem(self, name):
        self.nsem += 1
        return self.es.enter_context(self.nc.semaphore(f"s{self.nsem}_{name}"))

    def _wait(self, eng, evs):
        best = {}
        for ev in evs:
            if ev is None:
                continue
            sem, val = ev
            k = id(sem)
            if sem is eng.sem and (eng is self.PE or not SAFE_SAME_ENGINE):
                continue
            if eng.known.get(k, 0) >= val:
                continue
            if k not in best or best[k][1] < val:
                best[k] = (sem, val)
        for sem, val in best.values():
            eng.h.wait_ge(sem, val)
            eng.known[id(sem)] = val

    def _deps(self, R, W, skip_sem=None):
        evs = []
        for t in R:
            evs.append(t.b.w)
        for t in W:
            if not (skip_sem is not None and t.b.w is not None and t.b.w[0] is skip_sem):
                evs.append(t.b.w)
            evs.extend(t.b.r.values())
        return evs

    def _commit(self, ev, R, W):
        k = id(ev[0])
        for t in R:
            old = t.b.r.get(k)
            if old is None or old[1] < ev[1]:
                t.b.r[k] = ev
        for t in W:
            t.b.w = ev
            t.b.r = {}

    def op(self, eng, fn, R, W, sig=True):
        self._wait(eng, self._deps(R, W))
        ins = fn(eng.h)
        if sig:
            eng.cnt += 1
            ins.then_inc(eng.sem, 1)
            self._commit((eng.sem, eng.cnt), R, W)
        return ins

    def group(self, eng, fns, R, W):
        self._wait(eng, self._deps(R, W))
        ins = None
        for fn in fns:
            ins = fn(eng.h)
        eng.cnt += 1
        ins.then_inc(eng.sem, 1)
        self._commit((eng.sem, eng.cnt), R, W)

    def dma(self, eng, out, in_, sbt, R, W, **kw):
        b = sbt.b
        if b.dsem is None:
            b.dsem = self.newsem("d_" + b.name)
            self.dsems.append(b)
        self._wait(eng, self._deps(R, W, skip_sem=b.dsem))
        ins = eng.h.dma_start(out=out, in_=in_, **kw)
        b.dcnt += 16
        ins.then_inc(b.dsem, 16)
        self._commit((b.dsem, b.dcnt), R, W)

    def finish(self):
        for e in self.engs:
            evs = [(o.sem, o.cnt) for o in self.engs if o is not e and o.cnt > 0]
            evs += [(b.dsem, b.dcnt) for b in self.dsems]
            self._wait(e, evs)


class _Stop(Exception):
    pass


def build(SEQ, NB=2, dbg=None, stop_at=None):
    nc = bass.Bass("TRN2", target_bir_lowering=False)
    NT = SEQ // 128
    NS = SEQ // 512
    dr = lambda n, s, dt=F32, kind="ExternalInput": nc.dram_tensor(n, s, dt, kind=kind).ap()
    x_d = dr("x", [NB, SEQ, D])
    cT_d = dr("cT", [128, 8 * NB])
    wada_d = dr("w_ada", [D, 6 * D])
    win_d = dr("w_in", [D, 3608])
    wo_d = dr("w_o", [D, D])
    wup_d = dr("w_up", [D, 5632])
    wdn_d = dr("w_down", [2816, D])
    pp_d = dr("pp", [128, NP_COLS])
    vec_d = dr("vecs", [8, D])
    w2b_d = dr("w2b", [17, 256])
    cst_d = dr("cst", [128, 8 * 128])
    out_d = dr("out", [NB, SEQ, D], kind="ExternalOutput")
    wi16_d = dr("wi16", [7, 128, 8 * GW], BF16, kind="Internal")
    wu16_d = dr("wu16", [11, 128, 8 * 512], BF16, kind="Internal")
    xr_d = dr("xr_s", [NB * NT, 128, D], F32, kind="Internal")
    x1_d = dr("x1_s", [NB * NT, 128, D], F32, kind="Internal")
    dbg_d = {}

    es = contextlib.ExitStack()
    try:
      with es:
        fw = FW(nc, es)
        def chk(name):
            if stop_at == name:
                fw.finish()
                raise _Stop()
        PE, ACT, DVE, POOL, SP = fw.PE, fw.ACT, fw.DVE, fw.POOL, fw.SP
        cnt = [0]

        cur = [es]

        def sb(shape, dt, name=None):
            cnt[0] += 1
            name = name or f"t{cnt[0]}"
            return Tl(cur[0].enter_context(nc.sbuf_tensor("sb_" + name, shape, dt)), name)

        psb = [Tl(es.enter_context(nc.psum_tensor(f"ps{i}", [128, 512], F32)), f"ps{i}") for i in range(8)]
        psi = [0]

        def ps():
            psi[0] = (psi[0] + 1) % 8
            return psb[psi[0]]

        class Pool_:
            def __init__(self, n, shape, dt, nm):
                self.l = [sb(shape, dt, f"{nm}{i}") for i in range(n)]
                self.i = 0

            def get(self):
                self.i = (self.i + 1) % len(self.l)
                return self.l[self.i]

        def act(out, in_, func, R, W, **kw):
            fw.op(ACT, lambda h: h.activation(out=out, in_=in_, func=func, **kw), R, W)

        def tt(E, out, in0, in1, op, R, W):
            fw.op(E, lambda h: h.tensor_tensor(out=out, in0=in0, in1=in1, op=op), R, W)

        def ts(E, out, in0, s1, op0, R, W, s2=None, op1=None):
            if op1 is None:
                fw.op(E, lambda h: h.tensor_scalar(out=out, in0=in0, scalar1=s1, scalar2=None, op0=op0), R, W)
            else:
                fw.op(E, lambda h: h.tensor_scalar(out=out, in0=in0, scalar1=s1, scalar2=s2, op0=op0, op1=op1), R, W)

        def stt(E, out, in0, s, in1, op0, op1, R, W):
            fw.op(E, lambda h: h.scalar_tensor_tensor(out=out, in0=in0, scalar=s, in1=in1, op0=op0, op1=op1), R, W)

        def cp(E, out, in_, R, W):
            if E is ACT:
                act(out, in_, AF.Copy, R, W)
            else:
                fw.op(E, lambda h: h.tensor_copy(out=out, in_=in_), R, W)

        def mm(pst, specs, R):
            fns = [(lambda h, o=o, l=l, r=r, s=s, e=e: h.matmul(o, lhsT=l, rhs=r, start=s, stop=e)) for (o, l, r, s, e) in specs]
            fw.group(PE, fns, R, [pst])

        def tr(pst, specs, R):
            fns = [(lambda h, o=o, i=i, d=d: h.transpose(out=o, in_=i, identity=d)) for (o, i, d) in specs]
            fw.group(PE, fns, R, [pst])

        def dump(name, tl, ap, shape, dt=F32):
            if dbg is None or name not in dbg:
                return
            key = name
            n = 0
            while key in dbg_d:
                n += 1
                key = f"{name}.{n}"
            d = nc.dram_tensor("dbg_" + key.replace(".", "_"), shape, dt, kind="ExternalOutput").ap()
            dbg_d[key] = "dbg_" + key.replace(".", "_")
            fw.dma(SP, d, ap, tl, [tl], [])

        cst = sb([128, 1024], F32, "cst")
        fw.dma(SP, cst[:], cst_d[:, :], cst, [], [cst])
        IDF, U, LP, ONES, M1, M2, MU, UN16 = [cst[:, i * 128:(i + 1) * 128] for i in range(8)]
        pp = sb([128, NP_COLS], F32, "pp")
        fw.dma(SP, pp[:], pp_d[:, :], pp, [], [pp])
        w2b = sb([17, 256], F32, "w2b")
        fw.dma(SP, w2b[:], w2b_d[:, :], w2b, [], [w2b])
        cb = sb([128, 256], BF16, "cb")
        cp(DVE, cb[:, 0:128], IDF, [cst], [cb])
        cp(DVE, cb[:, 128:256], ONES, [cst], [cb])
        IDB, ONESB = cb[:, 0:128], cb[:, 128:256]
        sm = sb([128, 64], F32, "sm")
        fw.op(DVE, lambda h: h.memset(sm[:, 4:5], EPS), [], [sm])
        fw.op(DVE, lambda h: h.memset(sm[:, 5:6], 1.0), [], [sm])
        act(sm[:, 0:4], pp[:, O_NAL:O_NAL + 4], AF.Exp, [pp], [sm])
        ts(DVE, sm[:, 0:4], sm[:, 0:4], -1.0, ALU.mult, [sm], [sm])
        NEA, EPSC, ONEC = sm[:, 0:4], sm[:, 4:5], sm[:, 5:6]

        def bcast(row, name):
            t = sb([128, D], F32, name)
            fw.dma(SP, t[:], vec_d[row, :].partition_broadcast(128), t, [], [t])
            return t

        gt1 = [[sb([128, D], F32, f"gt1_{w}{b}") for b in range(NB)] for w in range(2)]
        mod = sb([128, 64], F32, "mod")
        GB = {}
        es_s = contextlib.ExitStack()
        with es_s:
            cur[0] = es_s
            cT = sb([128, 8 * NB], F32, "cT")
            fw.dma(SP, cT[:], cT_d[:, :], cT, [], [cT])
            cond = sb([128, 8 * NB], F32, "cond")
            act(cond[:], cT[:], AF.Silu, [cT], [cond])
            crep = sb([128, 8 * NB * 128], F32, "crep")
            for j in range(8 * NB):
                act(crep[:, j * 128:(j + 1) * 128], ONES, AF.Identity, [cst, cond], [crep], scale=cond[:, j:j + 1])
            modT = sb([128, 48 * NB], F32, "modT")
            bgt = [bcast(6, "bgta"), bcast(7, "bgtf")]
            wa_pool = Pool_(2, [128, 8 * 512], F32, "wa")
            for blk in range(12):
                wa = wa_pool.get()
                fw.dma(SP, wa[:].rearrange("p (c n) -> p c n", c=8),
                       wada_d.rearrange("(c p) n -> p c n", p=128)[:, :, blk * 512:(blk + 1) * 512], wa, [], [wa])
                sec = blk // 2
                if sec in (2, 5):
                    w = 0 if sec == 2 else 1
                    half = blk % 2
                    for b in range(NB):
                        p = ps()
                        mm(p, [(p[:, :], crep[:, (c * NB + b) * 128:(c * NB + b + 1) * 128], wa[:, c * 512:(c + 1) * 512], c == 0, c == 7)
                               for c in range(8)], [crep, wa])
                        stt(DVE, gt1[w][b][:, half * 512:(half + 1) * 512], p[:, :], 1.0, bgt[w][:, half * 512:(half + 1) * 512],
                            ALU.add, ALU.add, [p, bgt[w]], [gt1[w][b]])
                else:
                    p = ps()
                    specs = []
                    for m in range(4):
                        for c in range(8):
                            specs.append((p[:, m * NB:(m + 1) * NB], wa[:, c * 512 + m * 128: c * 512 + (m + 1) * 128],
                                          cond[:, c * NB:(c + 1) * NB], c == 0, c == 7))
                    mm(p, specs, [cond, wa])
                    j0 = blk * 4
                    for m in range(4):
                        ts(DVE, modT[:, (j0 + m) * NB:(j0 + m + 1) * NB], p[:, m * NB:(m + 1) * NB], pp[:, O_BADA + j0 + m:O_BADA + j0 + m + 1],
                           ALU.add, [p, pp], [modT])
            def modcol(j0, b):
                return modT[:].rearrange("p (j b) -> p b j", b=NB)[:, b, j0:j0 + 8]

            for w, (jsh, jsc, og, ob) in enumerate([(0, 8, O_G0, O_B0), (24, 32, O_G1, O_B1)]):
                for b in range(NB):
                    base = ((w * NB + b) * 2) * 8
                    Gc = mod[:, base:base + 8]
                    Bc = mod[:, base + 8:base + 16]
                    ts(DVE, Gc, modcol(jsc, b), 1.0, ALU.add, [modT], [mod])
                    tt(DVE, Bc, Gc, pp[:, ob:ob + 8], ALU.mult, [mod, pp], [mod])
                    tt(DVE, Bc, Bc, modcol(jsh, b), ALU.add, [mod, modT], [mod])
                    tt(DVE, Gc, Gc, pp[:, og:og + 8], ALU.mult, [mod, pp], [mod])
                    GB[(w, b)] = (base, base + 8)

            fw.finish()
            cur[0] = es
        chk('setup')

        stat = Pool_(4, [128, 32], F32, "stat")

        def ln_stats(zt, zap):
            st = stat.get()
            fw.op(DVE, lambda h: h.bn_stats(out=st[:, 0:6], in_=zap[:, 0:512]), [zt], [st])
            fw.op(DVE, lambda h: h.bn_stats(out=st[:, 6:12], in_=zap[:, 512:1024]), [zt], [st])
            fw.op(DVE, lambda h: h.bn_aggr(out=st[:, 12:14], in_=st[:, 0:12].rearrange("p (a b) -> p a b", a=2)), [st], [st])
            act(st[:, 14:15], st[:, 13:14], AF.Sqrt, [st, sm], [st], bias=EPSC, scale=1.0)
            fw.op(DVE, lambda h: h.reciprocal(out=st[:, 16:17], in_=st[:, 14:15]), [st], [st])
            stt(DVE, st[:, 17:18], st[:, 12:13], -1.0, st[:, 16:17], ALU.mult, ALU.mult, [st], [st])
            return st

        es_a = contextlib.ExitStack()
        _outer_es = es
        with es_a:
            def sba(shape, dt, name):
                return Tl(es_a.enter_context(nc.sbuf_tensor("sa_" + name, shape, dt)), name)

            class PoolA:
                def __init__(self, n, shape, dt, nm):
                    self.l = [sba(shape, dt, f"{nm}{i}") for i in range(n)]
                    self.i = 0

                def get(self):
                    self.i = (self.i + 1) % len(self.l)
                    return self.l[self.i]

            gA = sba([128, D], F32, "gA")
            bA = sba([128, D], F32, "bA")
            fw.dma(SP, gA[:], vec_d[0, :].partition_broadcast(128), gA, [], [gA])
            fw.dma(SP, bA[:], vec_d[1, :].partition_broadcast(128), bA, [], [bA])
            ts(DVE, gA[:], gA[:], ALPHA, ALU.mult, [gA], [gA])
            ts(DVE, bA[:], bA[:], ALPHA, ALU.mult, [bA], [bA])
            wo = sba([128, 8 * D], BF16, "wo")
            fw.dma(POOL, wo[:].rearrange("p (c n) -> p c n", c=8), wo_d.rearrange("(c p) n -> p c n", p=128), wo, [], [wo])
            wslot = PoolA(3, [128, 8 * GW], BF16, "wsl")
            wi16_b = [Tl(None, f"wi16_{g}") for g in range(7)]
            xt_pool = PoolA(2, [128, D], F32, "xt")
            hT = sba([128, 8 * 512], BF16, "hT")
            qkvT = sba([128, 12 * 512], BF16, "qkvT")
            zgT = sba([128, 8 * 512], BF16, "zgT")
            glqkT = sba([64, 8 * 512], BF16, "glqkT")
            glrT = sba([17, 512], F32, "glrT")
            glv = sba([128, 4 * 512], BF16, "glv")
            glkt = sba([128, 4 * 256], F32, "glkt")
            abt = sba([128, 32], F32, "abt")
            carry = sba([128, 12 * 3], F32, "carry")
            xc_pool = PoolA(2, [128, 515], F32, "xc")
            s32 = PoolA(8, [128, 512], F32, "s32_")
            s16 = PoolA(8, [128, 512], BF16, "s16_")
            ded16 = {n: sba([128, 512], BF16, "d_" + n) for n in ["Aoff", "At", "qdT", "TT", "kdec", "vnew", "r"]}
            vb = sba([128, 512], F32, "vb")
            S = sba([128, 512], F32, "S")
            Sb = sba([128, 512], BF16, "Sb")
            S2 = sba([64, 512], F32, "S2")
            S2b = sba([64, 512], BF16, "S2b")
            oT_pool = PoolA(2, [128, 1024], F32, "oT")
            z_pool = PoolA(2, [128, D], F32, "z")
            gsm = PoolA(2, [128, 64], F32, "gsm")
            fw.op(POOL, lambda h: h.memset(glrT[:, :], 1.0), [], [glrT])

            for b in range(NB):
                Gb, Bb = GB[(0, b)]
                fw.op(POOL, lambda h: h.memset(S[:], 0.0), [], [S])
                fw.op(POOL, lambda h: h.memset(Sb[:], 0.0), [], [Sb])
                fw.op(POOL, lambda h: h.memset(S2[:], 0.0), [], [S2])
                fw.op(POOL, lambda h: h.memset(S2b[:], 0.0), [], [S2b])
                fw.op(POOL, lambda h: h.memset(carry[:], 0.0), [], [carry])
                for s in range(NS):
                    first = (b == 0 and s == 0)
                    for t4 in range(4):
                        ti = b * NT + s * 4 + t4
                        xt = xt_pool.get()
                        fw.dma(SP, xt[:], x_d[b, (s * 4 + t4) * 128:(s * 4 + t4 + 1) * 128, :], xt, [], [xt])
                        st = ln_stats(xt, xt)
                        act(xt[:], xt[:], AF.Identity, [xt, st], [xt], scale=st[:, 16:17], bias=st[:, 17:18])
                        if t4 == 0:
                            dump("xn", xt, xt[:], [128, D])
                        for half in range(2):
                            p = ps()
                            tr(p, [(p[:, m * 128:(m + 1) * 128], xt[:, (half * 4 + m) * 128:(half * 4 + m + 1) * 128], IDF) for m in range(4)], [xt, cst])
                            for m in range(4):
                                c = half * 4 + m
                                E = ACT if m % 2 == 0 else DVE
                                o = hT[:, c * 512 + t4 * 128: c * 512 + (t4 + 1) * 128]
                                if E is ACT:
                                    act(o, p[:, m * 128:(m + 1) * 128], AF.Identity, [p, mod], [hT],
                                        scale=mod[:, Gb + c:Gb + c + 1], bias=mod[:, Bb + c:Bb + c + 1])
                                else:
                                    ts(DVE, o, p[:, m * 128:(m + 1) * 128], mod[:, Gb + c:Gb + c + 1], ALU.mult, [p, mod], [hT],
                                       s2=mod[:, Bb + c:Bb + c + 1], op1=ALU.add)
                        tt(POOL, xt[:], xt[:], gA[:], ALU.mult, [xt, gA], [xt])
                        tt(POOL, xt[:], xt[:], bA[:], ALU.add, [xt, bA], [xt])
                        xrb = Tl(None, f"xr{ti}")
                        xr_bufs[ti] = xrb
                        fw.dma(SP, xr_d[ti], xt[:], xt, [xt], [xrb])
                    dump("hT", hT, hT[:], [128, 4096], BF16)
                    chk("ln0")

                    def load_group(g):
                        c0, ncol = WI_GROUPS[g]
                        wsl = wslot.get()
                        w3 = wsl[:].rearrange("p (c n) -> p c n", c=8)
                        if first:
                            fw.dma(POOL, w3[:, :, 0:ncol], win_d.rearrange("(c p) n -> p c n", p=128)[:, :, c0:c0 + ncol], wsl, [], [wsl])
                            fw.dma(SP, wi16_d[g], wsl[:], wsl, [wsl], [wi16_b[g]])
                        else:
                            fw.dma(SP, wsl[:], wi16_d[g], wsl, [wi16_b[g]], [wsl])
                        return wsl

                    def fm_proj(wsl, col, M):
                        p = ps()
                        mm(p, [(p[0:M, :], wsl[:, c * GW + col: c * GW + col + M], hT[:, c * 512:(c + 1) * 512], c == 0, c == 7) for c in range(8)], [wsl, hT])
                        return p

                    def tm_proj(wsl, col, N, t4):
                        p = ps()
                        mm(p, [(p[:, 0:N], hT[:, c * 512 + t4 * 128: c * 512 + (t4 + 1) * 128], wsl[:, c * GW + col: c * GW + col + N], c == 0, c == 7) for c in range(8)], [wsl, hT])
                        return p

                    for g in range(3):
                        wsl = load_group(g)
                        for hh in range(4):
                            m = g * 4 + hh
                            p = fm_proj(wsl, hh * 128, 128)
                            xc = xc_pool.get()
                            cp(POOL, xc[:, 0:3], carry[:, m * 3:(m + 1) * 3], [carry], [xc])
                            cp(ACT, xc[:, 3:515], p[:, :], [p], [xc])
                            cp(POOL, carry[:, m * 3:(m + 1) * 3], xc[:, 512:515], [xc], [carry])
                            acc = s32.get()
                            wcol = lambda k: pp[:, O_DNC + k * 12 + m: O_DNC + k * 12 + m + 1]
                            ts(DVE, acc[:], xc[:, 3:515], wcol(3), ALU.mult, [xc, pp], [acc])
                            stt(DVE, acc[:], xc[:, 2:514], wcol(2), acc[:], ALU.mult, ALU.add, [xc, pp, acc], [acc])
                            stt(DVE, acc[:], xc[:, 1:513], wcol(1), acc[:], ALU.mult, ALU.add, [xc, pp, acc], [acc])
                            stt(DVE, acc[:], xc[:, 0:512], wcol(0), acc[:], ALU.mult, ALU.add, [xc, pp, acc], [acc])
                            dst = qkvT[:, m * 512:(m + 1) * 512]
                            if g == 2:
                                act(dst, acc[:], AF.Silu, [acc], [qkvT])
                            else:
                                act(acc[:], acc[:], AF.Silu, [acc], [acc])
                                sq = s16.get()
                                act(sq[:], acc[:], AF.Square, [acc], [sq])
                                p2 = ps()
                                mm(p2, [(p2[:, :], ONESB, sq[:], True, True)], [cb, sq])
                                rn = s32.get()
                                act(rn[:], p2[:, :], AF.Sqrt, [p2, sm], [rn], bias=EPSC, scale=1.0)
                                fw.op(DVE, lambda h: h.reciprocal(out=rn[:], in_=rn[:]), [rn], [rn])
                                if g == 0:
                                    stt(DVE, dst, acc[:], 128.0 ** -0.5, rn[:], ALU.mult, ALU.mult, [acc, rn], [qkvT])
                                else:
                                    tt(DVE, dst, acc[:], rn[:], ALU.mult, [acc, rn], [qkvT])
                    dump("qkvT", qkvT, qkvT[:], [128, 12 * 512], BF16)
                    chk("qkv")
                    wsl = load_group(3)
                    for hh in range(4):
                        p = fm_proj(wsl, hh * 128, 128)
                        tmp = s32.get()
                        act(tmp[:], p[:, :], AF.Silu, [p], [tmp])
                        ts(DVE, zgT[:, hh * 512:(hh + 1) * 512], tmp[:], pp[:, O_DNG:O_DNG + 1], ALU.mult, [tmp, pp], [zgT])
                    wsl = load_group(4)
                    for t4 in range(4):
                        p = tm_proj(wsl, 0, 8, t4)
                        cp(DVE, abt[:, t4 * 8:(t4 + 1) * 8], p[:, 0:8], [p], [abt])
                        p = tm_proj(wsl, 8 + 256, 256, t4)
                        cp(ACT, glkt[:, t4 * 256:(t4 + 1) * 256], p[:, 0:256], [p], [glkt])
                    for hq in range(8):
                        p = fm_proj(wsl, 8 + hq * 64, 64)
                        cp(ACT if hq % 2 else DVE, glqkT[:, hq * 512:(hq + 1) * 512], p[0:64, :], [p], [glqkT])
                    wsl = load_group(5)
                    for t4 in range(4):
                        p = tm_proj(wsl, 0, 512, t4)
                        cp(ACT if t4 % 2 else DVE, glv[:, t4 * 512:(t4 + 1) * 512], p[:, :], [p], [glv])
                    wsl = load_group(6)
                    for hh in range(4):
                        p = fm_proj(wsl, hh * 128, 128)
                        tmp = s32.get()
                        act(tmp[:], p[:, :], AF.Silu, [p], [tmp])
                        ts(DVE, zgT[:, (4 + hh) * 512:(5 + hh) * 512], tmp[:], pp[:, O_GLG:O_GLG + 1], ALU.mult, [tmp, pp], [zgT])
                    p = fm_proj(wsl, 512, 16)
                    cp(DVE, glrT[0:16, :], p[0:16, :], [p], [glrT])
                    dump("glv", glv, glv[:], [128, 2048], BF16)
                    dump("glqkT", glqkT, glqkT[:], [64, 4096], BF16)
                    chk("inproj")

                    for t4 in range(4):
                        ti = b * NT + s * 4 + t4
                        cs0 = t4 * 128
                        hsl = lambda base, h: slice(base + h * 512 + cs0, base + h * 512 + cs0 + 128)
                        qTh = lambda h: qkvT[:, hsl(0, h)]
                        kTh = lambda h: qkvT[:, hsl(4 * 512, h)]
                        vTh = lambda h: qkvT[:, hsl(8 * 512, h)]
                        H = lambda t, h: t[:, h * 128:(h + 1) * 128]
                        g = gsm.get()
                        tt(DVE, g[:, 8:12], abt[:, t4 * 8:t4 * 8 + 4], pp[:, O_DTB:O_DTB + 4], ALU.add, [abt, pp], [g])
                        act(g[:, 8:12], g[:, 8:12], AF.Exp, [g], [g])
                        act(g[:, 8:12], g[:, 8:12], AF.Ln, [g, sm], [g], bias=ONEC, scale=1.0)
                        tt(DVE, g[:, 0:4], g[:, 8:12], NEA, ALU.mult, [g, sm], [g])
                        act(g[:, 12:16], abt[:, t4 * 8 + 4:t4 * 8 + 8], AF.Exp, [abt], [g], scale=-1.0)
                        ts(DVE, g[:, 12:16], g[:, 12:16], 1.0, ALU.add, [g], [g])
                        fw.op(DVE, lambda h: h.reciprocal(out=g[:, 4:8], in_=g[:, 12:16]), [g], [g])
                        LA, BETA = g[:, 0:4], g[:, 4:8]
                        p = ps()
                        mm(p, [(p[:, 0:4], U, LA, True, True), (p[:, 4:8], LP, LA, True, True), (p[:, 8:12], ONES, LA, True, True)], [cst, g])
                        act(g[:, 16:28], p[:, 0:12], AF.Exp, [p], [g])
                        EG, EGR, GL = g[:, 16:20], g[:, 20:24], g[:, 24:28]
                        stt(DVE, g[:, 28:32], EG, -1.0, BETA, ALU.mult, ALU.mult, [g], [g])
                        NEBG = g[:, 28:32]
                        ula = s32.get()
                        for h in range(4):
                            ts(POOL, H(ula, h), U, LA[:, h:h + 1], ALU.mult, [cst, g], [ula])
                        pD = ps()
                        mm(pD, [(H(pD, h), H(ula, h), LP, True, True) for h in range(4)], [ula, cst])
                        pDt = ps()
                        mm(pDt, [(pDt[:, :], LP, ula[:], True, True)], [ula, cst])
                        pGr = ps()
                        mm(pGr, [(pGr[:, :], ONES, ula[:], True, True)], [ula, cst])
                        Et = s32.get()
                        act(Et[:], pDt[:, :], AF.Exp, [pDt], [Et])
                        eGrow = s32.get()
                        act(eGrow[:], pGr[:, :], AF.Exp, [pGr], [eGrow])
                        E = s32.get()
                        act(E[:], pD[:, :], AF.Exp, [pD], [E])
                        F1 = s32.get()
                        F2 = s32.get()
                        for h in range(4):
                            stt(DVE, H(F1, h), H(E, h), BETA[:, h:h + 1], M1, ALU.mult, ALU.mult, [E, g, cst], [F1])
                            stt(DVE, H(F2, h), H(E, h), BETA[:, h:h + 1], M2, ALU.mult, ALU.mult, [E, g, cst], [F2])
                            tt(POOL, H(Et, h), H(Et, h), MU, ALU.mult, [Et, cst], [Et])
                        pKK = ps()
                        mm(pKK, [(H(pKK, h), kTh(h), kTh(h), True, True) for h in range(4)], [qkvT])
                        pQK = ps()
                        mm(pQK, [(H(pQK, h), kTh(h), qTh(h), True, True) for h in range(4)], [qkvT])
                        X = s16.get()
                        tt(DVE, X[:], pKK[:, :], F1[:], ALU.mult, [pKK, F1], [X])
                        Aoff = ded16["Aoff"]
                        tt(DVE, Aoff[:], pKK[:, :], F2[:], ALU.mult, [pKK, F2], [Aoff])
                        At = ded16["At"]
                        tt(DVE, At[:], pQK[:, :], Et[:], ALU.mult, [pQK, Et], [At])
                        qdT = ded16["qdT"]
                        for h in range(4):
                            tt(POOL, H(qdT, h), qTh(h), H(eGrow, h), ALU.mult, [qkvT, eGrow], [qdT])
                        pk = ps()
                        pkb = pk[:].bitcast(BF16)
                        tr(pk, [(pkb[:, h * 128:(h + 1) * 128], kTh(h), IDB) for h in range(4)]
                           + [(pkb[:, 512 + h * 128:512 + (h + 1) * 128], vTh(h), IDB) for h in range(4)], [qkvT, cb])
                        kdec = ded16["kdec"]
                        for h in range(4):
                            ts(DVE, H(kdec, h), pkb[:, h * 128:(h + 1) * 128], EGR[:, h:h + 1], ALU.mult, [pk, g], [kdec])
                            ts(DVE, H(vb, h), pkb[:, 512 + h * 128:512 + (h + 1) * 128], BETA[:, h:h + 1], ALU.mult, [pk, g], [vb])
                        pT = ps()
                        pTb = pT[:].bitcast(BF16)
                        tr(pT, [(pTb[:, h * 128:(h + 1) * 128], H(X, h), IDB) for h in range(4)], [X, cb])
                        XT = s16.get()
                        cp(ACT, XT[:], pTb[:, 0:512], [pT], [XT])
                        PT = s16.get()
                        for h in range(4):
                            tt(DVE, H(PT, h), pTb[:, h * 128:(h + 1) * 128], IDB, ALU.add, [pT, cb], [PT])
                        for lvl in range(1, 6):
                            pX = ps()
                            mm(pX, [(H(pX, h), H(XT, h), H(X, h), True, True) for h in range(4)], [X, XT])
                            Xn = s16.get()
                            cp(ACT, Xn[:], pX[:, :], [pX], [Xn])
                            if lvl < 5:
                                pXT = ps()
                                mm(pXT, [(H(pXT, h), H(X, h), H(XT, h), True, True) for h in range(4)], [X, XT])
                                XTn = s16.get()
                                cp(DVE, XTn[:], pXT[:, :], [pXT], [XTn])
                            pP = ps()
                            specs = []
                            for h in range(4):
                                specs.append((H(pP, h), IDB, H(PT, h), True, False))
                                specs.append((H(pP, h), H(Xn, h), H(PT, h), False, True))
                            mm(pP, specs, [cb, PT, Xn])
                            PTn = s16.get()
                            cp(ACT if lvl % 2 else DVE, PTn[:], pP[:, :], [pP], [PTn])
                            X, PT = Xn, PTn
                            if lvl < 5:
                                XT = XTn
                        TbdT = PT
                        pT2 = ps()
                        pT2b = pT2[:].bitcast(BF16)
                        tr(pT2, [(pT2b[:, h * 128:(h + 1) * 128], H(TbdT, h), IDB) for h in range(4)], [TbdT, cb])
                        Tbd = s16.get()
                        cp(ACT, Tbd[:], pT2b[:, 0:512], [pT2], [Tbd])
                        pZ = ps()
                        mm(pZ, [(H(pZ, h), H(Aoff, h), H(TbdT, h), True, True) for h in range(4)], [Aoff, TbdT])
                        nZ = s16.get()
                        ts(DVE, nZ[:], pZ[:, :], -1.0, ALU.mult, [pZ], [nZ])
                        pTT = ps()
                        specs = []
                        for h in range(4):
                            specs.append((H(pTT, h), IDB, H(TbdT, h), True, False))
                            specs.append((H(pTT, h), H(Tbd, h), H(nZ, h), False, True))
                        mm(pTT, specs, [cb, TbdT, Tbd, nZ])
                        TT = ded16["TT"]
                        cp(ACT, TT[:], pTT[:, :], [pTT], [TT])
                        if t4 == 0:
                            dump("TT", TT, TT[:], [128, 512], BF16)
                            dump("At", At, At[:], [128, 512], BF16)
                        chk("dnpre")
                        pKS = ps()
                        mm(pKS, [(H(pKS, h), kTh(h), H(Sb, h), True, True) for h in range(4)], [qkvT, Sb])
                        r = ded16["r"]
                        for h in range(4):
                            stt(DVE, H(r, h), H(pKS, h), NEBG[:, h:h + 1], H(vb, h), ALU.mult, ALU.add, [pKS, g, vb], [r])
                        pV = ps()
                        mm(pV, [(H(pV, h), H(TT, h), H(r, h), True, True) for h in range(4)], [TT, r])
                        vnew = ded16["vnew"]
                        cp(ACT, vnew[:], pV[:, :], [pV], [vnew])
                        oT = oT_pool.get()
                        pO = ps()
                        specs = []
                        for h in range(4):
                            specs.append((H(pO, h), H(Sb, h), H(qdT, h), True, False))
                            specs.append((H(pO, h), H(vnew, h), H(At, h), False, True))
                        mm(pO, specs, [Sb, qdT, vnew, At])
                        cp(ACT, oT[:, 0:512], pO[:, :], [pO], [oT])
                        pS = ps()
                        mm(pS, [(H(pS, h), H(kdec, h), H(vnew, h), True, True) for h in range(4)], [kdec, vnew])
                        for h in range(4):
                            stt(DVE, H(S, h), H(S, h), GL[:, h:h + 1], H(pS, h), ALU.mult, ALU.add, [S, g, pS], [S])
                        cp(ACT, Sb[:], S[:], [S], [Sb])

                        chk("dnrec")
                        pL = ps()
                        mm(pL, [(pL[:, 0:256], glrT[0:17, cs0:cs0 + 128], w2b[:, :], True, True)], [glrT, w2b])
                        lt = s32.get()
                        act(lt[:, 0:256], pL[:, 0:256], AF.Exp, [pL], [lt], scale=-1.0)
                        act(lt[:, 0:256], lt[:, 0:256], AF.Ln, [lt, sm], [lt], bias=ONEC, scale=1.0)
                        pB = ps()
                        mm(pB, [(pB[:, 0:256], UN16, lt[:, 0:256], True, True)], [cst, lt])
                        pBT = ps()
                        mm(pBT, [(pBT[0:64, h * 128:(h + 1) * 128], lt[:, h * 64:(h + 1) * 64], UN16, True, True) for h in range(4)], [cst, lt])
                        ebT = s32.get()
                        act(ebT[0:64, :], pBT[0:64, :], AF.Exp, [pBT], [ebT])
                        enbT = s32.get()
                        act(enbT[0:64, :], pBT[0:64, :], AF.Exp, [pBT], [enbT], scale=-1.0)
                        enbt = s32.get()
                        act(enbt[:, 0:256], pB[:, 0:256], AF.Exp, [pB], [enbt], scale=-1.0)
                        qd2 = s16.get()
                        kn2 = s16.get()
                        for h in range(4):
                            stt(DVE, qd2[0:64, h * 128:(h + 1) * 128], glqkT[:, h * 512 + cs0:h * 512 + cs0 + 128], 0.125, ebT[0:64, h * 128:(h + 1) * 128],
                                ALU.mult, ALU.mult, [glqkT, ebT], [qd2])
                            tt(POOL, kn2[0:64, h * 128:(h + 1) * 128], glqkT[:, (4 + h) * 512 + cs0:(4 + h) * 512 + cs0 + 128], enbT[0:64, h * 128:(h + 1) * 128],
                               ALU.mult, [glqkT, enbT], [kn2])
                        knt = s16.get()
                        tt(DVE, knt[:, 0:256], glkt[:, t4 * 256:(t4 + 1) * 256], enbt[:, 0:256], ALU.mult, [glkt, enbt], [knt])
                        pA2 = ps()
                        mm(pA2, [(H(pA2, h), kn2[0:64, h * 128:(h + 1) * 128], qd2[0:64, h * 128:(h + 1) * 128], True, True) for h in range(4)], [kn2, qd2])
                        At2 = s16.get()
                        for h in range(4):
                            tt(DVE, H(At2, h), H(pA2, h), MU, ALU.mult, [pA2, cst], [At2])
                        pO2 = ps()
                        specs = []
                        for h in range(4):
                            specs.append((H(pO2, h), S2b[:, h * 128:(h + 1) * 128], qd2[0:64, h * 128:(h + 1) * 128], True, False))
                            specs.append((H(pO2, h), glv[:, t4 * 512 + h * 128:t4 * 512 + (h + 1) * 128], H(At2, h), False, True))
                        mm(pO2, specs, [S2b, qd2, glv, At2])
                        cp(DVE, oT[:, 512:1024], pO2[:, :], [pO2], [oT])
                        pS2 = ps()
                        mm(pS2, [(pS2[0:64, h * 128:(h + 1) * 128], knt[:, h * 64:(h + 1) * 64], glv[:, t4 * 512 + h * 128:t4 * 512 + (h + 1) * 128], True, True)
                                 for h in range(4)], [knt, glv])
                        tt(DVE, S2[:, :], S2[:, :], pS2[0:64, :], ALU.add, [S2, pS2], [S2])
                        for h in range(4):
                            ts(POOL, S2[:, h * 128:(h + 1) * 128], S2[:, h * 128:(h + 1) * 128], ebT[0:64, h * 128 + 127:h * 128 + 128], ALU.mult, [S2, ebT], [S2])
                        cp(ACT, S2b[:], S2[:], [S2], [S2b])
                        if t4 == 0:
                            dump("oT", oT, oT[:], [128, 1024])

                        chk("gla")
                        sqo = s16.get()
                        sqo2 = s16.get()
                        act(sqo[:], oT[:, 0:512], AF.Square, [oT], [sqo])
                        act(sqo2[:], oT[:, 512:1024], AF.Square, [oT], [sqo2])
                        for half, sq_ in enumerate((sqo, sqo2)):
                            p = ps()
                            mm(p, [(p[:, :], ONESB, sq_[:], True, True)], [cb, sq_])
                            rt = s32.get()
                            act(rt[:], p[:, :], AF.Sqrt, [p, sm], [rt], bias=EPSC, scale=1.0 / 128.0)
                            fw.op(DVE, lambda h: h.reciprocal(out=rt[:], in_=rt[:]), [rt], [rt])
                            tt(POOL, rt[:], rt[:], oT[:, half * 512:(half + 1) * 512], ALU.mult, [rt, oT], [rt])
                            zg3 = zgT[:, half * 2048:(half + 1) * 2048].rearrange("p (h n) -> p h n", h=4)[:, :, cs0:cs0 + 128]
                            og3 = hT[:, half * 2048:(half + 1) * 2048].rearrange("p (h n) -> p h n", h=4)[:, :, cs0:cs0 + 128]
                            tt(DVE, og3, rt[:].rearrange("p (h n) -> p h n", h=4), zg3, ALU.mult, [rt, zgT], [hT])

                    dump("ogT", hT, hT[:], [128, 4096], BF16)
                    chk("post")
                    for t4 in range(4):
                        ti = b * NT + s * 4 + t4
                        cs0 = t4 * 128
                        z = z_pool.get()
                        fw.dma(SP, z[:], xr_d[ti], z, [xr_bufs[ti]], [z])
                        for half in range(2):
                            p = ps()
                            mm(p, [(p[:, :], hT[:, c * 512 + cs0:c * 512 + cs0 + 128], wo[:, c * D + half * 512:c * D + (half + 1) * 512], c == 0, c == 7)
                                   for c in range(8)], [hT, wo])
                            tmp = s32.get()
                            tt(DVE, tmp[:], p[:, :], gt1[0][b][:, half * 512:(half + 1) * 512], ALU.mult, [p, gt1[0][b]], [tmp])
                            tt(POOL, z[:, half * 512:(half + 1) * 512], z[:, half * 512:(half + 1) * 512], tmp[:], ALU.add, [z, tmp], [z])
                        st = ln_stats(z, z)
                        act(z[:], z[:], AF.Identity, [z, st], [z], scale=st[:, 16:17], bias=st[:, 17:18])
                        if t4 == 0:
                            dump("x1n", z, z[:], [128, D])
                        x1b = Tl(None, f"x1_{ti}")
                        x1_bufs[ti] = x1b
                        fw.dma(SP, x1_d[ti], z[:], z, [z], [x1b])
            fw.finish()
        chk('phaseA')

        es_b = contextlib.ExitStack()
        with es_b:
            def sbb(shape, dt, name):
                return Tl(es_b.enter_context(nc.sbuf_tensor("sbb_" + name, shape, dt)), name)

            class PoolB:
                def __init__(self, n, shape, dt, nm):
                    self.l = [sbb(shape, dt, f"{nm}{i}") for i in range(n)]
                    self.i = 0

                def get(self):
                    self.i = (self.i + 1) % len(self.l)
                    return self.l[self.i]

            def bcastb(row, name, mul=None):
                t = sbb([128, D], F32, name)
                fw.dma(SP, t[:], vec_d[row, :].partition_broadcast(128), t, [], [t])
                if mul is not None:
                    ts(DVE, t[:], t[:], mul, ALU.mult, [t], [t])
                return t

            g1A = bcastb(2, "g1A", ALPHA)
            b1A = bcastb(3, "b1A", ALPHA)
            g2 = bcastb(4, "g2")
            b2 = bcastb(5, "b2")
            wd = sbb([128, 22 * D], BF16, "wd")
            for q in range(22):
                fw.dma(POOL, wd[:, q * D:(q + 1) * D], wdn_d[q * 128:(q + 1) * 128, :], wd, [], [wd])
            wslb = PoolB(3, [128, 8 * 512], BF16, "wslb")
            wu16_b = [Tl(None, f"wu16_{g}") for g in range(11)]
            x1_pool = PoolB(4, [128, D], F32, "x1t")
            hfT = sbb([128, 8 * 512], BF16, "hfT")
            aT = sbb([128, 22 * 512], BF16, "aT")
            uc_pool = PoolB(3, [128, 514], F32, "uc")
            acc_pool = PoolB(3, [128, 512], F32, "accb")
            carryf = sbb([128, 44 * 2], F32, "carryf")
            tmp_pool = PoolB(2, [128, 512], F32, "tmpb")
            chk('b_load')
            for b in range(NB):
                Gb, Bb = GB[(1, b)]
                fw.op(POOL, lambda h: h.memset(carryf[:], 0.0), [], [carryf])
                for s in range(NS):
                    first = (b == 0 and s == 0)
                    x1t = []
                    for t4 in range(4):
                        ti = b * NT + s * 4 + t4
                        xt = x1_pool.get()
                        x1t.append(xt)
                        fw.dma(SP, xt[:], x1_d[ti], xt, [x1_bufs[ti]], [xt])
                        for half in range(2):
                            p = ps()
                            tr(p, [(p[:, m * 128:(m + 1) * 128], xt[:, (half * 4 + m) * 128:(half * 4 + m + 1) * 128], IDF) for m in range(4)], [xt, cst])
                            for m in range(4):
                                c = half * 4 + m
                                o = hfT[:, c * 512 + t4 * 128: c * 512 + (t4 + 1) * 128]
                                if m % 2 == 0:
                                    act(o, p[:, m * 128:(m + 1) * 128], AF.Identity, [p, mod], [hfT],
                                        scale=mod[:, Gb + c:Gb + c + 1], bias=mod[:, Bb + c:Bb + c + 1])
                                else:
                                    ts(DVE, o, p[:, m * 128:(m + 1) * 128], mod[:, Gb + c:Gb + c + 1], ALU.mult, [p, mod], [hfT],
                                       s2=mod[:, Bb + c:Bb + c + 1], op1=ALU.add)
                        tt(POOL, xt[:], xt[:], g1A[:], ALU.mult, [xt, g1A], [xt])
                        tt(POOL, xt[:], xt[:], b1A[:], ALU.add, [xt, b1A], [xt])
                    chk('b_hf')
                    for g in range(11):
                        wsl = wslb.get()
                        if first:
                            fw.dma(POOL, wsl[:].rearrange("p (c n) -> p c n", c=8),
                                   wup_d.rearrange("(c p) n -> p c n", p=128)[:, :, g * 512:(g + 1) * 512], wsl, [], [wsl])
                            fw.dma(SP, wu16_d[g], wsl[:], wsl, [wsl], [wu16_b[g]])
                        else:
                            fw.dma(SP, wsl[:], wu16_d[g], wsl, [wu16_b[g]], [wsl])
                        chk(f"b_g{g}_load")
                        for mi in range(4):
                            M = g * 4 + mi
                            p = ps()
                            mm(p, [(p[:, :], wsl[:, c * 512 + mi * 128:c * 512 + (mi + 1) * 128], hfT[:, c * 512:(c + 1) * 512], c == 0, c == 7) for c in range(8)], [wsl, hfT])
                            uc = uc_pool.get()
                            cp(POOL, uc[:, 0:2], carryf[:, M * 2:M * 2 + 2], [carryf], [uc])
                            cp(ACT, uc[:, 2:514], p[:, :], [p], [uc])
                            cp(POOL, carryf[:, M * 2:M * 2 + 2], uc[:, 512:514], [uc], [carryf])
                            acc = acc_pool.get()
                            wc = lambda k: pp[:, O_FC + k * 44 + M:O_FC + k * 44 + M + 1]
                            ts(DVE, acc[:], uc[:, 2:514], wc(2), ALU.mult, [uc, pp], [acc], s2=pp[:, O_FB + M:O_FB + M + 1], op1=ALU.add)
                            stt(DVE, acc[:], uc[:, 1:513], wc(1), acc[:], ALU.mult, ALU.add, [uc, pp, acc], [acc])
                            stt(DVE, acc[:], uc[:, 0:512], wc(0), acc[:], ALU.mult, ALU.add, [uc, pp, acc], [acc])
                            chk(f"b_m{M}_conv")
                            if M < 22:
                                act(aT[:, M * 512:(M + 1) * 512], acc[:], AF.Silu, [acc], [aT])
                            else:
                                Mg = M - 22
                                tt(DVE, aT[:, Mg * 512:(Mg + 1) * 512], aT[:, Mg * 512:(Mg + 1) * 512], acc[:], ALU.mult, [aT, acc], [aT])
                    dump("aT", aT, aT[:, 0:2048], [128, 2048], BF16)
                    chk("b_up")
                    for t4 in range(4):
                        ti = b * NT + s * 4 + t4
                        cs0 = t4 * 128
                        xt = x1t[t4]
                        for half in range(2):
                            p = ps()
                            mm(p, [(p[:, :], aT[:, c * 512 + cs0:c * 512 + cs0 + 128], wd[:, c * D + half * 512:c * D + (half + 1) * 512], c == 0, c == 21)
                                   for c in range(22)], [aT, wd])
                            tmp = tmp_pool.get()
                            tt(DVE, tmp[:], p[:, :], gt1[1][b][:, half * 512:(half + 1) * 512], ALU.mult, [p, gt1[1][b]], [tmp])
                            tt(POOL, xt[:, half * 512:(half + 1) * 512], xt[:, half * 512:(half + 1) * 512], tmp[:], ALU.add, [xt, tmp], [xt])
                        st = ln_stats(xt, xt)
                        act(xt[:], xt[:], AF.Identity, [xt, st], [xt], scale=st[:, 16:17], bias=st[:, 17:18])
                        tt(DVE, xt[:], xt[:], g2[:], ALU.mult, [xt, g2], [xt])
                        tt(POOL, xt[:], xt[:], b2[:], ALU.add, [xt, b2], [xt])
                        ob = Tl(None, f"out{ti}")
                        fw.dma(SP, out_d[b, (s * 4 + t4) * 128:(s * 4 + t4 + 1) * 128, :], xt[:], xt, [xt], [ob])
            fw.finish()
    except _Stop:
        pass
    return nc, dbg_d


x1_bufs = {}
xr_bufs = {}


def _consts():
    i = np.arange(128)
    k = i[:, None]
    j = i[None, :]
    ident = (k == j).astype(np.float32)
    Uc = (k <= j).astype(np.float32)
    Lp = (k > j).astype(np.float32)
    ones = np.ones((128, 128), np.float32)
    same = (k // 64) == (j // 64)
    M1 = -((k > j) & same).astype(np.float32)
    M2 = ((k >= 64) & (j < 64)).astype(np.float32)
    MU = (k <= j).astype(np.float32)
    UN16 = -Uc / 16.0
    return np.concatenate([ident, Uc, Lp, ones, M1, M2, MU, UN16], axis=1).astype(np.float32)


def _prep(inputs, core, NB):
    f = lambda a: np.ascontiguousarray(np.asarray(a, dtype=np.float32))
    b0 = core * NB
    c = f(inputs["c"])[b0:b0 + NB]
    cT = c.reshape(NB, 8, 128).transpose(2, 1, 0).reshape(128, 8 * NB)
    fm = lambda v: f(v).reshape(-1, 128).T
    pp = np.zeros((128, NP_COLS), np.float32)
    pp[:, O_G0:O_G0 + 8] = fm(inputs["ln0_g"])
    pp[:, O_B0:O_B0 + 8] = fm(inputs["ln0_b"])
    pp[:, O_G1:O_G1 + 8] = fm(inputs["ln1_g"][0])
    pp[:, O_B1:O_B1 + 8] = fm(inputs["ln1_b"][0])
    pp[:, O_BADA:O_BADA + 48] = fm(inputs["b_ada"][0])
    dc = f(inputs["dn_conv"][0])
    for k in range(4):
        pp[:, O_DNC + k * 12:O_DNC + (k + 1) * 12] = fm(dc[k])
    pp[:, O_DNG] = f(inputs["dn_norm_g"][0])
    pp[:, O_GLG] = f(inputs["gla_norm_g"][0])
    fc = f(inputs["ffn_conv"][0])
    for k in range(3):
        pp[:, O_FC + k * 44:O_FC + (k + 1) * 44] = fm(fc[k])
    pp[:, O_FB:O_FB + 44] = fm(inputs["ffn_conv_b"][0])
    pp[:, O_NAL:O_NAL + 4] = f(inputs["dn_a_log"][0])[None, :]
    pp[:, O_DTB:O_DTB + 4] = f(inputs["dn_dt_bias"][0])[None, :]
    ba = f(inputs["b_ada"][0])
    vecs = np.stack([f(inputs["ln0_g"]), f(inputs["ln0_b"]), f(inputs["ln1_g"][0]), f(inputs["ln1_b"][0]),
                     f(inputs["ln2_g"][0]), f(inputs["ln2_b"][0]), ba[2048:3072], ba[5120:6144]], axis=0)
    w2b = np.concatenate([f(inputs["gla_w_gate2"][0]), f(inputs["gla_b_gate"][0])[None, :]], axis=0)
    return {
        "x": f(inputs["x"])[b0:b0 + NB],
        "cT": np.ascontiguousarray(cT),
        "w_ada": f(inputs["w_ada"][0]),
        "w_in": f(inputs["w_in"][0]),
        "w_o": f(inputs["w_o"][0]),
        "w_up": f(inputs["ffn_w_up"][0]),
        "w_down": f(inputs["ffn_w_down"][0]),
        "pp": pp,
        "vecs": np.ascontiguousarray(vecs),
        "w2b": np.ascontiguousarray(w2b),
        "cst": _consts(),
    }


def run(inputs, n_cores=8, dbg=None, stop_at=None):
    x = np.asarray(inputs["x"])
    B, SEQ, _ = x.shape
    NB = B // n_cores
    x1_bufs.clear()
    xr_bufs.clear()
    nc, dbg_d = build(SEQ, NB, dbg, stop_at)
    in_maps = [_prep(inputs, core, NB) for core in range(n_cores)]
    res = run_bass_kernel_spmd(nc, in_maps, core_ids=list(range(n_cores)))
    out = np.concatenate([np.asarray(r["out"]) for r in res.results], axis=0).astype(np.float32)
    return out, res, dbg_d


def kernel(**inputs):
    out, _, _ = run(inputs, 8)
    return out
```

```python
import contextlib
import os
import numpy as np
import concourse.bass as bass
import concourse.mybir as mybir
from concourse.bass_utils import run_bass_kernel_spmd

F32 = mybir.dt.float32
BF16 = mybir.dt.bfloat16
AF = mybir.ActivationFunctionType
ALU = mybir.AluOpType

D = 1024
ALPHA = 2.0 ** 0.25
EPS = 1e-6
NP_COLS = 320
O_G0, O_B0, O_G1, O_B1, O_BADA, O_DNC, O_DNG, O_GLG, O_FC, O_FB, O_NAL, O_DTB = 0, 8, 16, 24, 32, 80, 128, 129, 130, 262, 306, 310
WI_GROUPS = [(0, 512), (512, 512), (1024, 512), (1536, 512), (2048, 520), (2568, 512), (3080, 528)]
GW = 528
SAFE_SAME_ENGINE = True


class Buf:
    __slots__ = ("name", "w", "r", "dsem", "dcnt")

    def __init__(self, name):
        self.name = name
        self.w = None
        self.r = {}
        self.dsem = None
        self.dcnt = 0


class Tl:
    def __init__(self, t, name, psum=False):
        self.t = t
        self.b = Buf(name)
        self.psum = psum

    def __getitem__(self, k):
        return self.t[k]


class Eng:
    def __init__(self, h, sem, name):
        self.h = h
        self.sem = sem
        self.cnt = 0
        self.known = {}
        self.name = name


class FW:
    def __init__(self, nc, es):
        self.nc = nc
        self.es = es
        self.nsem = 0
        self.PE = Eng(nc.tensor, self.newsem("pe"), "pe")
        self.ACT = Eng(nc.scalar, self.newsem("act"), "act")
        self.DVE = Eng(nc.vector, self.newsem("dve"), "dve")
        self.POOL = Eng(nc.gpsimd, self.newsem("pool"), "pool")
        self.SP = Eng(nc.sync, self.newsem("sp"), "sp")
        self.engs = [self.PE, self.ACT, self.DVE, self.POOL, self.SP]
        self.dsems = []

    def newsem(self, name):
        self.nsem += 1
        return self.es.enter_context(self.nc.semaphore(f"s{self.nsem}_{name}"))

    def _wait(self, eng, evs):
        best = {}
        for ev in evs:
            if ev is None:
                continue
            sem, val = ev
            k = id(sem)
            if sem is eng.sem and (eng is self.PE or not SAFE_SAME_ENGINE):
                continue
            if eng.known.get(k, 0) >= val:
                continue
            if k not in best or best[k][1] < val:
                best[k] = (sem, val)
        for sem, val in best.values():
            eng.h.wait_ge(sem, val)
            eng.known[id(sem)] = val

    def _deps(self, R, W, skip_sem=None):
        evs = []
        for t in R:
            evs.append(t.b.w)
            if t.psum:
                evs.extend(t.b.r.values())
        for t in W:
            if not (skip_sem is not None and t.b.w is not None and t.b.w[0] is skip_sem):
                evs.append(t.b.w)
            evs.extend(t.b.r.values())
        return evs

    def _commit(self, ev, R, W):
        k = id(ev[0])
        for t in R:
            old = t.b.r.get(k)
            if old is None or old[1] < ev[1]:
                t.b.r[k] = ev
        for t in W:
            t.b.w = ev
            t.b.r = {}

    def op(self, eng, fn, R, W, sig=True):
        self._wait(eng, self._deps(R, W))
        ins = fn(eng.h)
        if sig:
            eng.cnt += 1
            ins.then_inc(eng.sem, 1)
            self._commit((eng.sem, eng.cnt), R, W)
        return ins

    def group(self, eng, fns, R, W):
        self._wait(eng, self._deps(R, W))
        ins = None
        for fn in fns:
            ins = fn(eng.h)
        eng.cnt += 1
        ins.then_inc(eng.sem, 1)
        self._commit((eng.sem, eng.cnt), R, W)

    def dma(self, eng, out, in_, sbt, R, W, **kw):
        b = sbt.b
        if b.dsem is None:
            b.dsem = self.newsem("d_" + b.name)
            self.dsems.append(b)
        self._wait(eng, self._deps(R, W, skip_sem=b.dsem))
        ins = eng.h.dma_start(out=out, in_=in_, **kw)
        b.dcnt += 16
        ins.then_inc(b.dsem, 16)
        self._commit((b.dsem, b.dcnt), R, W)

    def finish(self):
        for e in self.engs:
            evs = [(o.sem, o.cnt) for o in self.engs if o is not e and o.cnt > 0]
            evs += [(b.dsem, b.dcnt) for b in self.dsems]
            self._wait(e, evs)


class _Stop(Exception):
    pass


def build(SEQ, NB=2, dbg=None, stop_at=None):
    nc = bass.Bass("TRN2", target_bir_lowering=False)
    NT = SEQ // 128
    NS = SEQ // 512
    dr = lambda n, s, dt=F32, kind="ExternalInput": nc.dram_tensor(n, s, dt, kind=kind).ap()
    x_d = dr("x", [NB, SEQ, D])
    cT_d = dr("cT", [128, 8 * NB])
    wada_d = dr("w_ada", [D, 6 * D])
    win_d = dr("w_in", [D, 3608])
    wo_d = dr("w_o", [D, D])
    wup_d = dr("w_up", [D, 5632])
    wdn_d = dr("w_down", [2816, D])
    pp_d = dr("pp", [128, NP_COLS])
    vec_d = dr("vecs", [8, D])
    w2b_d = dr("w2b", [17, 256])
    cst_d = dr("cst", [128, 8 * 128])
    out_d = dr("out", [NB, SEQ, D], kind="ExternalOutput")
    wi16_d = dr("wi16", [7, 128, 8 * GW], BF16, kind="Internal")
    wu16_d = dr("wu16", [11, 128, 8 * 512], BF16, kind="Internal")
    xr_d = dr("xr_s", [NB * NT, 128, D], F32, kind="Internal")
    x1_d = dr("x1_s", [NB * NT, 128, D], F32, kind="Internal")
    dbg_d = {}

    es = contextlib.ExitStack()
    try:
      with es:
        fw = FW(nc, es)
        def chk(name):
            if stop_at == name:
                fw.finish()
                raise _Stop()
        PE, ACT, DVE, POOL, SP = fw.PE, fw.ACT, fw.DVE, fw.POOL, fw.SP
        cnt = [0]

        cur = [es]

        def sb(shape, dt, name=None):
            cnt[0] += 1
            name = name or f"t{cnt[0]}"
            return Tl(cur[0].enter_context(nc.sbuf_tensor("sb_" + name, shape, dt)), name)

        psb = [Tl(es.enter_context(nc.psum_tensor(f"ps{i}", [128, 512], F32)), f"ps{i}", psum=True) for i in range(8)]
        psi = [0]

        def ps():
            psi[0] = (psi[0] + 1) % 8
            return psb[psi[0]]

        class Pool_:
            def __init__(self, n, shape, dt, nm):
                self.l = [sb(shape, dt, f"{nm}{i}") for i in range(n)]
                self.i = 0

            def get(self):
                self.i = (self.i + 1) % len(self.l)
                return self.l[self.i]

        def act(out, in_, func, R, W, **kw):
            fw.op(ACT, lambda h: h.activation(out=out, in_=in_, func=func, **kw), R, W)

        def tt(E, out, in0, in1, op, R, W):
            fw.op(E, lambda h: h.tensor_tensor(out=out, in0=in0, in1=in1, op=op), R, W)

        def ts(E, out, in0, s1, op0, R, W, s2=None, op1=None):
            if op1 is None:
                fw.op(E, lambda h: h.tensor_scalar(out=out, in0=in0, scalar1=s1, scalar2=None, op0=op0), R, W)
            else:
                fw.op(E, lambda h: h.tensor_scalar(out=out, in0=in0, scalar1=s1, scalar2=s2, op0=op0, op1=op1), R, W)

        def stt(E, out, in0, s, in1, op0, op1, R, W):
            fw.op(E, lambda h: h.scalar_tensor_tensor(out=out, in0=in0, scalar=s, in1=in1, op0=op0, op1=op1), R, W)

        def cp(E, out, in_, R, W):
            if E is ACT:
                act(out, in_, AF.Copy, R, W)
            else:
                fw.op(E, lambda h: h.tensor_copy(out=out, in_=in_), R, W)

        def mm(pst, specs, R):
            fns = [(lambda h, o=o, l=l, r=r, s=s, e=e: h.matmul(o, lhsT=l, rhs=r, start=s, stop=e)) for (o, l, r, s, e) in specs]
            fw.group(PE, fns, R, [pst])

        def tr(pst, specs, R):
            fns = [(lambda h, o=o, i=i, d=d: h.transpose(out=o, in_=i, identity=d)) for (o, i, d) in specs]
            fw.group(PE, fns, R, [pst])

        def dump(name, tl, ap, shape, dt=F32):
            if dbg is None or name not in dbg:
                return
            key = name
            n = 0
            while key in dbg_d:
                n += 1
                key = f"{name}.{n}"
            d = nc.dram_tensor("dbg_" + key.replace(".", "_"), shape, dt, kind="ExternalOutput").ap()
            dbg_d[key] = "dbg_" + key.replace(".", "_")
            fw.dma(SP, d, ap, tl, [tl], [])

        cst = sb([128, 1024], F32, "cst")
        fw.dma(SP, cst[:], cst_d[:, :], cst, [], [cst])
        IDF, U, LP, ONES, M1, M2, MU, UN16 = [cst[:, i * 128:(i + 1) * 128] for i in range(8)]
        pp = sb([128, NP_COLS], F32, "pp")
        fw.dma(SP, pp[:], pp_d[:, :], pp, [], [pp])
        w2b = sb([17, 256], F32, "w2b")
        fw.dma(SP, w2b[:], w2b_d[:, :], w2b, [], [w2b])
        cb = sb([128, 256], BF16, "cb")
        cp(DVE, cb[:, 0:128], IDF, [cst], [cb])
        cp(DVE, cb[:, 128:256], ONES, [cst], [cb])
        IDB, ONESB = cb[:, 0:128], cb[:, 128:256]
        sm = sb([128, 64], F32, "sm")
        fw.op(DVE, lambda h: h.memset(sm[:, 4:5], EPS), [], [sm])
        fw.op(DVE, lambda h: h.memset(sm[:, 5:6], 1.0), [], [sm])
        act(sm[:, 0:4], pp[:, O_NAL:O_NAL + 4], AF.Exp, [pp], [sm])
        ts(DVE, sm[:, 0:4], sm[:, 0:4], -1.0, ALU.mult, [sm], [sm])
        NEA, EPSC, ONEC = sm[:, 0:4], sm[:, 4:5], sm[:, 5:6]

        def bcast(row, name):
            t = sb([128, D], F32, name)
            fw.dma(SP, t[:], vec_d[row, :].partition_broadcast(128), t, [], [t])
            return t

        gt1 = [[sb([128, D], F32, f"gt1_{w}{b}") for b in range(NB)] for w in range(2)]
        mod = sb([128, 64], F32, "mod")
        GB = {}
        es_s = contextlib.ExitStack()
        with es_s:
            cur[0] = es_s
            cT = sb([128, 8 * NB], F32, "cT")
            fw.dma(SP, cT[:], cT_d[:, :], cT, [], [cT])
            cond = sb([128, 8 * NB], F32, "cond")
            act(cond[:], cT[:], AF.Silu, [cT], [cond])
            crep = sb([128, 8 * NB * 128], F32, "crep")
            for j in range(8 * NB):
                act(crep[:, j * 128:(j + 1) * 128], ONES, AF.Identity, [cst, cond], [crep], scale=cond[:, j:j + 1])
            modT = sb([128, 48 * NB], F32, "modT")
            bgt = [bcast(6, "bgta"), bcast(7, "bgtf")]
            wa_pool = Pool_(2, [128, 8 * 512], F32, "wa")
            for blk in range(12):
                wa = wa_pool.get()
                fw.dma(SP, wa[:].rearrange("p (c n) -> p c n", c=8),
                       wada_d.rearrange("(c p) n -> p c n", p=128)[:, :, blk * 512:(blk + 1) * 512], wa, [], [wa])
                sec = blk // 2
                if sec in (2, 5):
                    w = 0 if sec == 2 else 1
                    half = blk % 2
                    for b in range(NB):
                        p = ps()
                        mm(p, [(p[:, :], crep[:, (c * NB + b) * 128:(c * NB + b + 1) * 128], wa[:, c * 512:(c + 1) * 512], c == 0, c == 7)
                               for c in range(8)], [crep, wa])
                        stt(DVE, gt1[w][b][:, half * 512:(half + 1) * 512], p[:, :], 1.0, bgt[w][:, half * 512:(half + 1) * 512],
                            ALU.add, ALU.add, [p, bgt[w]], [gt1[w][b]])
                else:
                    p = ps()
                    specs = []
                    for m in range(4):
                        for c in range(8):
                            specs.append((p[:, m * NB:(m + 1) * NB], wa[:, c * 512 + m * 128: c * 512 + (m + 1) * 128],
                                          cond[:, c * NB:(c + 1) * NB], c == 0, c == 7))
                    mm(p, specs, [cond, wa])
                    j0 = blk * 4
                    for m in range(4):
                        ts(DVE, modT[:, (j0 + m) * NB:(j0 + m + 1) * NB], p[:, m * NB:(m + 1) * NB], pp[:, O_BADA + j0 + m:O_BADA + j0 + m + 1],
                           ALU.add, [p, pp], [modT])
            def modcol(j0, b):
                return modT[:].rearrange("p (j b) -> p b j", b=NB)[:, b, j0:j0 + 8]

            for w, (jsh, jsc, og, ob) in enumerate([(0, 8, O_G0, O_B0), (24, 32, O_G1, O_B1)]):
                for b in range(NB):
                    base = ((w * NB + b) * 2) * 8
                    Gc = mod[:, base:base + 8]
                    Bc = mod[:, base + 8:base + 16]
                    ts(DVE, Gc, modcol(jsc, b), 1.0, ALU.add, [modT], [mod])
                    tt(DVE, Bc, Gc, pp[:, ob:ob + 8], ALU.mult, [mod, pp], [mod])
                    tt(DVE, Bc, Bc, modcol(jsh, b), ALU.add, [mod, modT], [mod])
                    tt(DVE, Gc, Gc, pp[:, og:og + 8], ALU.mult, [mod, pp], [mod])
                    GB[(w, b)] = (base, base + 8)

            fw.finish()
            cur[0] = es
        chk('setup')

        stat = Pool_(4, [128, 32], F32, "stat")

        def ln_stats(zt, zap):
            st = stat.get()
            fw.op(DVE, lambda h: h.bn_stats(out=st[:, 0:6], in_=zap[:, 0:512]), [zt], [st])
            fw.op(DVE, lambda h: h.bn_stats(out=st[:, 6:12], in_=zap[:, 512:1024]), [zt], [st])
            fw.op(DVE, lambda h: h.bn_aggr(out=st[:, 12:14], in_=st[:, 0:12].rearrange("p (a b) -> p a b", a=2)), [st], [st])
            act(st[:, 14:15], st[:, 13:14], AF.Sqrt, [st, sm], [st], bias=EPSC, scale=1.0)
            fw.op(DVE, lambda h: h.reciprocal(out=st[:, 16:17], in_=st[:, 14:15]), [st], [st])
            stt(DVE, st[:, 17:18], st[:, 12:13], -1.0, st[:, 16:17], ALU.mult, ALU.mult, [st], [st])
            return st

        es_a = contextlib.ExitStack()
        _outer_es = es
        with es_a:
            def sba(shape, dt, name):
                return Tl(es_a.enter_context(nc.sbuf_tensor("sa_" + name, shape, dt)), name)

            class PoolA:
                def __init__(self, n, shape, dt, nm):
                    self.l = [sba(shape, dt, f"{nm}{i}") for i in range(n)]
                    self.i = 0

                def get(self):
                    self.i = (self.i + 1) % len(self.l)
                    return self.l[self.i]

            gA = sba([128, D], F32, "gA")
            bA = sba([128, D], F32, "bA")
            fw.dma(SP, gA[:], vec_d[0, :].partition_broadcast(128), gA, [], [gA])
            fw.dma(SP, bA[:], vec_d[1, :].partition_broadcast(128), bA, [], [bA])
            ts(DVE, gA[:], gA[:], ALPHA, ALU.mult, [gA], [gA])
            ts(DVE, bA[:], bA[:], ALPHA, ALU.mult, [bA], [bA])
            wo = sba([128, 8 * D], BF16, "wo")
            fw.dma(POOL, wo[:].rearrange("p (c n) -> p c n", c=8), wo_d.rearrange("(c p) n -> p c n", p=128), wo, [], [wo])
            wslot = PoolA(2, [128, 8 * GW], BF16, "wsl")
            wi16_b = [Tl(None, f"wi16_{g}") for g in range(7)]
            xt_pool = PoolA(2, [128, D], F32, "xt")
            hT = sba([128, 8 * 512], BF16, "hT")
            qkvT = sba([128, 12 * 512], BF16, "qkvT")
            zgT = sba([128, 8 * 512], BF16, "zgT")
            glqkT = sba([64, 8 * 512], BF16, "glqkT")
            glrT = sba([17, 512], F32, "glrT")
            glv = sba([128, 4 * 512], BF16, "glv")
            glkt = sba([128, 4 * 256], F32, "glkt")
            abt = sba([128, 32], F32, "abt")
            carry = sba([128, 12 * 3], F32, "carry")
            xc_pool = PoolA(2, [128, 515], F32, "xc")
            s32 = PoolA(3, [128, 512], F32, "s32_")
            s16 = PoolA(2, [128, 512], BF16, "s16_")

            class Ctx:
                def __init__(self, banks, n32, n16, tag):
                    self.banks = banks
                    self.bi = 0
                    self.p32 = PoolA(n32, [128, 512], F32, tag + "f") if n32 else None
                    self.p16 = PoolA(n16, [128, 512], BF16, tag + "h") if n16 else None

                def ps(self):
                    self.bi = (self.bi + 1) % len(self.banks)
                    return self.banks[self.bi]

            cpre = Ctx(psb[0:3], 6, 7, "cp")
            crec = Ctx(psb[3:5], 0, 0, "cr")
            cgla = Ctx(psb[5:7], 4, 4, "cg")
            cpost = Ctx(psb[7:8], 2, 2, "co")
            pset = []
            for i in range(2):
                dct = {n: sba([128, 512], BF16, f"d{i}_{n}") for n in ["Aoff", "At", "qdT", "TT", "kdec"]}
                dct["vb"] = sba([128, 512], F32, f"d{i}_vb")
                dct["g"] = sba([128, 64], F32, f"d{i}_g")
                pset.append(dct)
            vnew = sba([128, 512], BF16, "vnew")
            r_t = sba([128, 512], BF16, "r_t")
            S = sba([128, 512], F32, "S")
            Sb = sba([128, 512], BF16, "Sb")
            S2 = sba([64, 512], F32, "S2")
            S2b = sba([64, 512], BF16, "S2b")
            oT_pool = PoolA(2, [128, 1024], F32, "oT")
            z_pool = PoolA(2, [128, D], F32, "z")
            fw.op(POOL, lambda h: h.memset(glrT[:, :], 1.0), [], [glrT])

            for b in range(NB):
                Gb, Bb = GB[(0, b)]
                fw.op(POOL, lambda h: h.memset(S[:], 0.0), [], [S])
                fw.op(POOL, lambda h: h.memset(Sb[:], 0.0), [], [Sb])
                fw.op(POOL, lambda h: h.memset(S2[:], 0.0), [], [S2])
                fw.op(POOL, lambda h: h.memset(S2b[:], 0.0), [], [S2b])
                fw.op(POOL, lambda h: h.memset(carry[:], 0.0), [], [carry])
                for s in range(NS):
                    first = (b == 0 and s == 0)
                    for t4 in range(4):
                        ti = b * NT + s * 4 + t4
                        xt = xt_pool.get()
                        fw.dma(SP, xt[:], x_d[b, (s * 4 + t4) * 128:(s * 4 + t4 + 1) * 128, :], xt, [], [xt])
                        st = ln_stats(xt, xt)
                        act(xt[:], xt[:], AF.Identity, [xt, st], [xt], scale=st[:, 16:17], bias=st[:, 17:18])
                        if t4 == 0:
                            dump("xn", xt, xt[:], [128, D])
                        for half in range(2):
                            p = ps()
                            tr(p, [(p[:, m * 128:(m + 1) * 128], xt[:, (half * 4 + m) * 128:(half * 4 + m + 1) * 128], IDF) for m in range(4)], [xt, cst])
                            for m in range(4):
                                c = half * 4 + m
                                E = ACT if half == 0 else DVE
                                o = hT[:, c * 512 + t4 * 128: c * 512 + (t4 + 1) * 128]
                                if E is ACT:
                                    act(o, p[:, m * 128:(m + 1) * 128], AF.Identity, [p, mod], [hT],
                                        scale=mod[:, Gb + c:Gb + c + 1], bias=mod[:, Bb + c:Bb + c + 1])
                                else:
                                    ts(DVE, o, p[:, m * 128:(m + 1) * 128], mod[:, Gb + c:Gb + c + 1], ALU.mult, [p, mod], [hT],
                                       s2=mod[:, Bb + c:Bb + c + 1], op1=ALU.add)
                        tt(POOL, xt[:], xt[:], gA[:], ALU.mult, [xt, gA], [xt])
                        tt(POOL, xt[:], xt[:], bA[:], ALU.add, [xt, bA], [xt])
                        xrb = Tl(None, f"xr{ti}")
                        xr_bufs[ti] = xrb
                        fw.dma(SP, xr_d[ti], xt[:], xt, [xt], [xrb])
                    dump("hT", hT, hT[:], [128, 4096], BF16)
                    chk("ln0")

                    def load_group(g):
                        c0, ncol = WI_GROUPS[g]
                        wsl = wslot.get()
                        w3 = wsl[:].rearrange("p (c n) -> p c n", c=8)
                        if first:
                            fw.dma(POOL, w3[:, :, 0:ncol], win_d.rearrange("(c p) n -> p c n", p=128)[:, :, c0:c0 + ncol], wsl, [], [wsl])
                            fw.dma(SP, wi16_d[g], wsl[:], wsl, [wsl], [wi16_b[g]])
                        else:
                            fw.dma(SP, wsl[:], wi16_d[g], wsl, [wi16_b[g]], [wsl])
                        return wsl

                    def fm_proj(wsl, col, M):
                        p = ps()
                        mm(p, [(p[0:M, :], wsl[:, c * GW + col: c * GW + col + M], hT[:, c * 512:(c + 1) * 512], c == 0, c == 7) for c in range(8)], [wsl, hT])
                        return p

                    def tm_proj(wsl, col, N, t4):
                        p = ps()
                        mm(p, [(p[:, 0:N], hT[:, c * 512 + t4 * 128: c * 512 + (t4 + 1) * 128], wsl[:, c * GW + col: c * GW + col + N], c == 0, c == 7) for c in range(8)], [wsl, hT])
                        return p

                    for g in range(3):
                        wsl = load_group(g)
                        for hh in range(4):
                            m = g * 4 + hh
                            p = fm_proj(wsl, hh * 128, 128)
                            xc = xc_pool.get()
                            cp(POOL, xc[:, 0:3], carry[:, m * 3:(m + 1) * 3], [carry], [xc])
                            cp(ACT, xc[:, 3:515], p[:, :], [p], [xc])
                            cp(POOL, carry[:, m * 3:(m + 1) * 3], xc[:, 512:515], [xc], [carry])
                            acc = s32.get()
                            wcol = lambda k: pp[:, O_DNC + k * 12 + m: O_DNC + k * 12 + m + 1]
                            ts(DVE, acc[:], xc[:, 3:515], wcol(3), ALU.mult, [xc, pp], [acc])
                            stt(DVE, acc[:], xc[:, 2:514], wcol(2), acc[:], ALU.mult, ALU.add, [xc, pp, acc], [acc])
                            stt(DVE, acc[:], xc[:, 1:513], wcol(1), acc[:], ALU.mult, ALU.add, [xc, pp, acc], [acc])
                            stt(DVE, acc[:], xc[:, 0:512], wcol(0), acc[:], ALU.mult, ALU.add, [xc, pp, acc], [acc])
                            dst = qkvT[:, m * 512:(m + 1) * 512]
                            if g == 2:
                                act(dst, acc[:], AF.Silu, [acc], [qkvT])
                            else:
                                act(acc[:], acc[:], AF.Silu, [acc], [acc])
                                sq = s16.get()
                                act(sq[:], acc[:], AF.Square, [acc], [sq])
                                p2 = ps()
                                mm(p2, [(p2[:, :], ONESB, sq[:], True, True)], [cb, sq])
                                rn = s32.get()
                                act(rn[:], p2[:, :], AF.Ln, [p2, sm], [rn], bias=EPSC, scale=1.0)
                                act(rn[:], rn[:], AF.Exp, [rn], [rn], scale=-0.5)
                                if g == 0:
                                    stt(DVE, dst, acc[:], 128.0 ** -0.5, rn[:], ALU.mult, ALU.mult, [acc, rn], [qkvT])
                                else:
                                    tt(DVE, dst, acc[:], rn[:], ALU.mult, [acc, rn], [qkvT])
                    dump("qkvT", qkvT, qkvT[:], [128, 12 * 512], BF16)
                    chk("qkv")
                    wsl = load_group(3)
                    for hh in range(4):
                        p = fm_proj(wsl, hh * 128, 128)
                        tmp = s32.get()
                        act(tmp[:], p[:, :], AF.Silu, [p], [tmp])
                        ts(DVE, zgT[:, hh * 512:(hh + 1) * 512], tmp[:], pp[:, O_DNG:O_DNG + 1], ALU.mult, [tmp, pp], [zgT])
                    wsl = load_group(4)
                    for t4 in range(4):
                        p = tm_proj(wsl, 0, 8, t4)
                        cp(DVE, abt[:, t4 * 8:(t4 + 1) * 8], p[:, 0:8], [p], [abt])
                        p = tm_proj(wsl, 8 + 256, 256, t4)
                        cp(ACT, glkt[:, t4 * 256:(t4 + 1) * 256], p[:, 0:256], [p], [glkt])
                    for hq in range(8):
                        p = fm_proj(wsl, 8 + hq * 64, 64)
                        cp(ACT if hq % 2 else DVE, glqkT[:, hq * 512:(hq + 1) * 512], p[0:64, :], [p], [glqkT])
                    wsl = load_group(5)
                    for t4 in range(4):
                        p = tm_proj(wsl, 0, 512, t4)
                        cp(ACT if t4 % 2 else DVE, glv[:, t4 * 512:(t4 + 1) * 512], p[:, :], [p], [glv])
                    wsl = load_group(6)
                    for hh in range(4):
                        p = fm_proj(wsl, hh * 128, 128)
                        tmp = s32.get()
                        act(tmp[:], p[:, :], AF.Silu, [p], [tmp])
                        ts(DVE, zgT[:, (4 + hh) * 512:(5 + hh) * 512], tmp[:], pp[:, O_GLG:O_GLG + 1], ALU.mult, [tmp, pp], [zgT])
                    p = fm_proj(wsl, 512, 16)
                    cp(DVE, glrT[0:16, :], p[0:16, :], [p], [glrT])
                    dump("glv", glv, glv[:], [128, 2048], BF16)
                    dump("glqkT", glqkT, glqkT[:], [64, 4096], BF16)
                    chk("inproj")

                    H = lambda t, h: t[:, h * 128:(h + 1) * 128]

                    def hsl(base, h, cs0):
                        return slice(base + h * 512 + cs0, base + h * 512 + cs0 + 128)

                    def gen_pre(t4, P, C):
                        cs0 = t4 * 128
                        qTh = lambda h: qkvT[:, hsl(0, h, cs0)]
                        kTh = lambda h: qkvT[:, hsl(4 * 512, h, cs0)]
                        vTh = lambda h: qkvT[:, hsl(8 * 512, h, cs0)]
                        g = P["g"]
                        tt(DVE, g[:, 8:12], abt[:, t4 * 8:t4 * 8 + 4], pp[:, O_DTB:O_DTB + 4], ALU.add, [abt, pp], [g])
                        act(g[:, 8:12], g[:, 8:12], AF.Exp, [g], [g])
                        act(g[:, 8:12], g[:, 8:12], AF.Ln, [g, sm], [g], bias=ONEC, scale=1.0)
                        tt(DVE, g[:, 0:4], g[:, 8:12], NEA, ALU.mult, [g, sm], [g])
                        act(g[:, 12:16], abt[:, t4 * 8 + 4:t4 * 8 + 8], AF.Exp, [abt], [g], scale=-1.0)
                        ts(DVE, g[:, 12:16], g[:, 12:16], 1.0, ALU.add, [g], [g])
                        fw.op(DVE, lambda h: h.reciprocal(out=g[:, 4:8], in_=g[:, 12:16]), [g], [g])
                        LA, BETA = g[:, 0:4], g[:, 4:8]
                        yield
                        p = C.ps()
                        mm(p, [(p[:, 0:4], U, LA, True, True), (p[:, 4:8], LP, LA, True, True), (p[:, 8:12], ONES, LA, True, True)], [cst, g])
                        act(g[:, 16:28], p[:, 0:12], AF.Exp, [p], [g])
                        EG, EGR, GL = g[:, 16:20], g[:, 20:24], g[:, 24:28]
                        stt(DVE, g[:, 28:32], EG, -1.0, BETA, ALU.mult, ALU.mult, [g], [g])
                        ula = C.p32.get()
                        for h in range(4):
                            ts(POOL, H(ula, h), U, LA[:, h:h + 1], ALU.mult, [cst, g], [ula])
                        yield
                        pD = C.ps()
                        mm(pD, [(H(pD, h), H(ula, h), LP, True, True) for h in range(4)], [ula, cst])
                        pDt = C.ps()
                        mm(pDt, [(pDt[:, :], LP, ula[:], True, True)], [ula, cst])
                        pGr = C.ps()
                        mm(pGr, [(pGr[:, :], ONES, ula[:], True, True)], [ula, cst])
                        E = C.p32.get()
                        act(E[:], pD[:, :], AF.Exp, [pD], [E])
                        Et = C.p32.get()
                        act(Et[:], pDt[:, :], AF.Exp, [pDt], [Et])
                        eGrow = C.p32.get()
                        act(eGrow[:], pGr[:, :], AF.Exp, [pGr], [eGrow])
                        yield
                        F1 = C.p32.get()
                        F2 = C.p32.get()
                        for h in range(4):
                            stt(DVE, H(F1, h), H(E, h), BETA[:, h:h + 1], M1, ALU.mult, ALU.mult, [E, g, cst], [F1])
                            stt(DVE, H(F2, h), H(E, h), BETA[:, h:h + 1], M2, ALU.mult, ALU.mult, [E, g, cst], [F2])
                            tt(POOL, H(Et, h), H(Et, h), MU, ALU.mult, [Et, cst], [Et])
                        yield
                        pKK = C.ps()
                        mm(pKK, [(H(pKK, h), kTh(h), kTh(h), True, True) for h in range(4)], [qkvT])
                        pQK = C.ps()
                        mm(pQK, [(H(pQK, h), kTh(h), qTh(h), True, True) for h in range(4)], [qkvT])
                        X = C.p16.get()
                        tt(DVE, X[:], pKK[:, :], F1[:], ALU.mult, [pKK, F1], [X])
                        Aoff = P["Aoff"]
                        tt(DVE, Aoff[:], pKK[:, :], F2[:], ALU.mult, [pKK, F2], [Aoff])
                        At = P["At"]
                        tt(DVE, At[:], pQK[:, :], Et[:], ALU.mult, [pQK, Et], [At])
                        qdT = P["qdT"]
                        for h in range(4):
                            tt(POOL, H(qdT, h), qTh(h), H(eGrow, h), ALU.mult, [qkvT, eGrow], [qdT])
                        yield
                        pT = C.ps()
                        pTb = pT[:].bitcast(BF16)
                        tr(pT, [(pTb[:, h * 128:(h + 1) * 128], H(X, h), IDB) for h in range(4)], [X, cb])
                        XT = C.p16.get()
                        cp(ACT, XT[:], pTb[:, 0:512], [pT], [XT])
                        PT = C.p16.get()
                        for h in range(4):
                            tt(DVE, H(PT, h), H(XT, h), IDB, ALU.add, [XT, cb], [PT])
                        yield
                        pk = C.ps()
                        pkb = pk[:].bitcast(BF16)
                        tr(pk, [(pkb[:, h * 128:(h + 1) * 128], kTh(h), IDB) for h in range(4)]
                           + [(pkb[:, 512 + h * 128:512 + (h + 1) * 128], vTh(h), IDB) for h in range(4)], [qkvT, cb])
                        kdec = P["kdec"]
                        vb = P["vb"]
                        for h in range(4):
                            ts(DVE, H(kdec, h), pkb[:, h * 128:(h + 1) * 128], EGR[:, h:h + 1], ALU.mult, [pk, g], [kdec])
                            ts(DVE, H(vb, h), pkb[:, 512 + h * 128:512 + (h + 1) * 128], BETA[:, h:h + 1], ALU.mult, [pk, g], [vb])
                        yield
                        for lvl in range(1, 6):
                            pX = C.ps()
                            mm(pX, [(H(pX, h), H(XT, h), H(X, h), True, True) for h in range(4)], [X, XT])
                            if lvl < 5:
                                pXT = C.ps()
                                mm(pXT, [(H(pXT, h), H(X, h), H(XT, h), True, True) for h in range(4)], [X, XT])
                            Xn = C.p16.get()
                            cp(ACT, Xn[:], pX[:, :], [pX], [Xn])
                            if lvl < 5:
                                XTn = C.p16.get()
                                cp(DVE, XTn[:], pXT[:, :], [pXT], [XTn])
                            yield
                            pP = C.ps()
                            specs = []
                            for h in range(4):
                                specs.append((H(pP, h), IDB, H(PT, h), True, False))
                                specs.append((H(pP, h), H(Xn, h), H(PT, h), False, True))
                            mm(pP, specs, [cb, PT, Xn])
                            PTn = C.p16.get()
                            cp(ACT if lvl % 2 else DVE, PTn[:], pP[:, :], [pP], [PTn])
                            X, PT = Xn, PTn
                            if lvl < 5:
                                XT = XTn
                            yield
                        TbdT = PT
                        pT2 = C.ps()
                        pT2b = pT2[:].bitcast(BF16)
                        tr(pT2, [(pT2b[:, h * 128:(h + 1) * 128], H(TbdT, h), IDB) for h in range(4)], [TbdT, cb])
                        pZ = C.ps()
                        mm(pZ, [(H(pZ, h), H(P["Aoff"], h), H(TbdT, h), True, True) for h in range(4)], [P["Aoff"], TbdT])
                        Tbd = C.p16.get()
                        cp(ACT, Tbd[:], pT2b[:, 0:512], [pT2], [Tbd])
                        nZ = C.p16.get()
                        ts(DVE, nZ[:], pZ[:, :], -1.0, ALU.mult, [pZ], [nZ])
                        yield
                        pTT = C.ps()
                        specs = []
                        for h in range(4):
                            specs.append((H(pTT, h), IDB, H(TbdT, h), True, False))
                            specs.append((H(pTT, h), H(Tbd, h), H(nZ, h), False, True))
                        mm(pTT, specs, [cb, TbdT, Tbd, nZ])
                        TT = P["TT"]
                        cp(ACT, TT[:], pTT[:, :], [pTT], [TT])
                        yield

                    def gen_rec(t4, P, C, oT):
                        cs0 = t4 * 128
                        kTh = lambda h: qkvT[:, hsl(4 * 512, h, cs0)]
                        g = P["g"]
                        GL, NEBG = g[:, 24:28], g[:, 28:32]
                        vb, TT, At, qdT, kdec = P["vb"], P["TT"], P["At"], P["qdT"], P["kdec"]
                        pKS = C.ps()
                        mm(pKS, [(H(pKS, h), kTh(h), H(Sb, h), True, True) for h in range(4)], [qkvT, Sb])
                        for h in range(4):
                            stt(DVE, H(r_t, h), H(pKS, h), NEBG[:, h:h + 1], H(vb, h), ALU.mult, ALU.add, [pKS, g, vb], [r_t])
                        yield
                        pV = C.ps()
                        mm(pV, [(H(pV, h), H(TT, h), H(r_t, h), True, True) for h in range(4)], [TT, r_t])
                        cp(ACT, vnew[:], pV[:, :], [pV], [vnew])
                        yield
                        pO = C.ps()
                        specs = []
                        for h in range(4):
                            specs.append((H(pO, h), H(Sb, h), H(qdT, h), True, False))
                            specs.append((H(pO, h), H(vnew, h), H(At, h), False, True))
                        mm(pO, specs, [Sb, qdT, vnew, At])
                        pS = C.ps()
                        mm(pS, [(H(pS, h), H(kdec, h), H(vnew, h), True, True) for h in range(4)], [kdec, vnew])
                        cp(ACT, oT[:, 0:512], pO[:, :], [pO], [oT])
                        for h in range(4):
                            stt(DVE, H(S, h), H(S, h), GL[:, h:h + 1], H(pS, h), ALU.mult, ALU.add, [S, g, pS], [S])
                        cp(ACT, Sb[:], S[:], [S], [Sb])
                        yield

                    def gen_gla(t4, C, oT):
                        cs0 = t4 * 128
                        pL = C.ps()
                        mm(pL, [(pL[:, 0:256], glrT[0:17, cs0:cs0 + 128], w2b[:, :], True, True)], [glrT, w2b])
                        lt = C.p32.get()
                        act(lt[:, 0:256], pL[:, 0:256], AF.Exp, [pL], [lt], scale=-1.0)
                        act(lt[:, 0:256], lt[:, 0:256], AF.Ln, [lt, sm], [lt], bias=ONEC, scale=1.0)
                        yield
                        pB = C.ps()
                        mm(pB, [(pB[:, 0:256], UN16, lt[:, 0:256], True, True)], [cst, lt])
                        pBT = C.ps()
                        mm(pBT, [(pBT[0:64, h * 128:(h + 1) * 128], lt[:, h * 64:(h + 1) * 64], UN16, True, True) for h in range(4)], [cst, lt])
                        ebT = C.p32.get()
                        act(ebT[0:64, :], pBT[0:64, :], AF.Exp, [pBT], [ebT])
                        enbT = C.p32.get()
                        act(enbT[0:64, :], pBT[0:64, :], AF.Exp, [pBT], [enbT], scale=-1.0)
                        enbt = C.p32.get()
                        act(enbt[:, 0:256], pB[:, 0:256], AF.Exp, [pB], [enbt], scale=-1.0)
                        yield
                        qd2 = C.p16.get()
                        kn2 = C.p16.get()
                        for h in range(4):
                            stt(DVE, qd2[0:64, h * 128:(h + 1) * 128], glqkT[:, h * 512 + cs0:h * 512 + cs0 + 128], 0.125, ebT[0:64, h * 128:(h + 1) * 128],
                                ALU.mult, ALU.mult, [glqkT, ebT], [qd2])
                            tt(POOL, kn2[0:64, h * 128:(h + 1) * 128], glqkT[:, (4 + h) * 512 + cs0:(4 + h) * 512 + cs0 + 128], enbT[0:64, h * 128:(h + 1) * 128],
                               ALU.mult, [glqkT, enbT], [kn2])
                        knt = C.p16.get()
                        tt(DVE, knt[:, 0:256], glkt[:, t4 * 256:(t4 + 1) * 256], enbt[:, 0:256], ALU.mult, [glkt, enbt], [knt])
                        yield
                        pA2 = C.ps()
                        mm(pA2, [(H(pA2, h), kn2[0:64, h * 128:(h + 1) * 128], qd2[0:64, h * 128:(h + 1) * 128], True, True) for h in range(4)], [kn2, qd2])
                        At2 = C.p16.get()
                        for h in range(4):
                            tt(DVE, H(At2, h), H(pA2, h), MU, ALU.mult, [pA2, cst], [At2])
                        yield
                        pO2 = C.ps()
                        specs = []
                        for h in range(4):
                            specs.append((H(pO2, h), S2b[:, h * 128:(h + 1) * 128], qd2[0:64, h * 128:(h + 1) * 128], True, False))
                            specs.append((H(pO2, h), glv[:, t4 * 512 + h * 128:t4 * 512 + (h + 1) * 128], H(At2, h), False, True))
                        mm(pO2, specs, [S2b, qd2, glv, At2])
                        pS2 = C.ps()
                        mm(pS2, [(pS2[0:64, h * 128:(h + 1) * 128], knt[:, h * 64:(h + 1) * 64], glv[:, t4 * 512 + h * 128:t4 * 512 + (h + 1) * 128], True, True)
                                 for h in range(4)], [knt, glv])
                        cp(DVE, oT[:, 512:1024], pO2[:, :], [pO2], [oT])
                        tt(DVE, S2[:, :], S2[:, :], pS2[0:64, :], ALU.add, [S2, pS2], [S2])
                        for h in range(4):
                            ts(POOL, S2[:, h * 128:(h + 1) * 128], S2[:, h * 128:(h + 1) * 128], ebT[0:64, h * 128 + 127:h * 128 + 128], ALU.mult, [S2, ebT], [S2])
                        cp(ACT, S2b[:], S2[:], [S2], [S2b])
                        yield

                    def gen_post(t4, C, oT):
                        cs0 = t4 * 128
                        if t4 == 0:
                            dump("oT", oT, oT[:], [128, 1024])
                        for half in range(2):
                            sq_ = C.p16.get()
                            act(sq_[:], oT[:, half * 512:(half + 1) * 512], AF.Square, [oT], [sq_])
                            p = C.ps()
                            mm(p, [(p[:, :], ONESB, sq_[:], True, True)], [cb, sq_])
                            rt = C.p32.get()
                            if os.environ.get("POSTMODE", "0") == "1":
                                act(rt[:], p[:, :], AF.Sqrt, [p, sm], [rt], bias=EPSC, scale=1.0 / 128.0)
                                fw.op(DVE, lambda h: h.reciprocal(out=rt[:], in_=rt[:]), [rt], [rt])
                            else:
                                act(rt[:], p[:, :], AF.Ln, [p, sm], [rt], bias=EPSC, scale=1.0 / 128.0)
                                act(rt[:], rt[:], AF.Exp, [rt], [rt], scale=-0.5)
                            yield
                            tt(DVE, rt[:], rt[:], oT[:, half * 512:(half + 1) * 512], ALU.mult, [rt, oT], [rt])
                            zg3 = zgT[:, half * 2048:(half + 1) * 2048].rearrange("p (h n) -> p h n", h=4)[:, :, cs0:cs0 + 128]
                            og3 = hT[:, half * 2048:(half + 1) * 2048].rearrange("p (h n) -> p h n", h=4)[:, :, cs0:cs0 + 128]
                            tt(DVE, og3, rt[:].rearrange("p (h n) -> p h n", h=4), zg3, ALU.mult, [rt, zgT], [hT])
                            yield

                    def merge(gens):
                        gens = list(gens)
                        while gens:
                            for gq in list(gens):
                                try:
                                    next(gq)
                                except StopIteration:
                                    gens.remove(gq)

                    oTs = {}
                    for rr in range(6):
                        gl_ = []
                        if rr < 4:
                            gl_.append(gen_pre(rr, pset[rr % 2], cpre))
                        if 1 <= rr <= 4:
                            oTs[rr - 1] = oT_pool.get()
                            gl_.append(gen_rec(rr - 1, pset[(rr - 1) % 2], crec, oTs[rr - 1]))
                            gl_.append(gen_gla(rr - 1, cgla, oTs[rr - 1]))
                        if 2 <= rr <= 5:
                            gl_.append(gen_post(rr - 2, cpost, oTs[rr - 2]))
                        merge(gl_)
                        chk(f'round{rr}')

                    dump("ogT", hT, hT[:], [128, 4096], BF16)
                    chk("post")
                    for t4 in range(4):
                        ti = b * NT + s * 4 + t4
                        cs0 = t4 * 128
                        z = z_pool.get()
                        fw.dma(SP, z[:], xr_d[ti], z, [xr_bufs[ti]], [z])
                        for half in range(2):
                            p = ps()
                            mm(p, [(p[:, :], hT[:, c * 512 + cs0:c * 512 + cs0 + 128], wo[:, c * D + half * 512:c * D + (half + 1) * 512], c == 0, c == 7)
                                   for c in range(8)], [hT, wo])
                            tmp = s32.get()
                            tt(DVE, tmp[:], p[:, :], gt1[0][b][:, half * 512:(half + 1) * 512], ALU.mult, [p, gt1[0][b]], [tmp])
                            tt(POOL, z[:, half * 512:(half + 1) * 512], z[:, half * 512:(half + 1) * 512], tmp[:], ALU.add, [z, tmp], [z])
                        st = ln_stats(z, z)
                        act(z[:], z[:], AF.Identity, [z, st], [z], scale=st[:, 16:17], bias=st[:, 17:18])
                        if t4 == 0:
                            dump("x1n", z, z[:], [128, D])
                        x1b = Tl(None, f"x1_{ti}")
                        x1_bufs[ti] = x1b
                        fw.dma(SP, x1_d[ti], z[:], z, [z], [x1b])
            fw.finish()
        chk('phaseA')

        es_b = contextlib.ExitStack()
        with es_b:
            def sbb(shape, dt, name):
                return Tl(es_b.enter_context(nc.sbuf_tensor("sbb_" + name, shape, dt)), name)

            class PoolB:
                def __init__(self, n, shape, dt, nm):
                    self.l = [sbb(shape, dt, f"{nm}{i}") for i in range(n)]
                    self.i = 0

                def get(self):
                    self.i = (self.i + 1) % len(self.l)
                    return self.l[self.i]

            def bcastb(row, name, mul=None):
                t = sbb([128, D], F32, name)
                fw.dma(SP, t[:], vec_d[row, :].partition_broadcast(128), t, [], [t])
                if mul is not None:
                    ts(DVE, t[:], t[:], mul, ALU.mult, [t], [t])
                return t

            g1A = bcastb(2, "g1A", ALPHA)
            b1A = bcastb(3, "b1A", ALPHA)
            g2 = bcastb(4, "g2")
            b2 = bcastb(5, "b2")
            wd = sbb([128, 22 * D], BF16, "wd")
            for q in range(22):
                fw.dma(POOL, wd[:, q * D:(q + 1) * D], wdn_d[q * 128:(q + 1) * 128, :], wd, [], [wd])
            wslb = PoolB(3, [128, 8 * 512], BF16, "wslb")
            wu16_b = [Tl(None, f"wu16_{g}") for g in range(11)]
            x1_pool = PoolB(4, [128, D], F32, "x1t")
            hfT = sbb([128, 8 * 512], BF16, "hfT")
            aT = sbb([128, 22 * 512], BF16, "aT")
            uc_pool = PoolB(3, [128, 514], F32, "uc")
            acc_pool = PoolB(3, [128, 512], F32, "accb")
            carryf = sbb([128, 44 * 2], F32, "carryf")
            tmp_pool = PoolB(2, [128, 512], F32, "tmpb")
            chk('b_load')
            for b in range(NB):
                Gb, Bb = GB[(1, b)]
                fw.op(POOL, lambda h: h.memset(carryf[:], 0.0), [], [carryf])
                for s in range(NS):
                    first = (b == 0 and s == 0)
                    x1t = []
                    for t4 in range(4):
                        ti = b * NT + s * 4 + t4
                        xt = x1_pool.get()
                        x1t.append(xt)
                        fw.dma(SP, xt[:], x1_d[ti], xt, [x1_bufs[ti]], [xt])
                        for half in range(2):
                            p = ps()
                            tr(p, [(p[:, m * 128:(m + 1) * 128], xt[:, (half * 4 + m) * 128:(half * 4 + m + 1) * 128], IDF) for m in range(4)], [xt, cst])
                            for m in range(4):
                                c = half * 4 + m
                                o = hfT[:, c * 512 + t4 * 128: c * 512 + (t4 + 1) * 128]
                                if half == 0:
                                    act(o, p[:, m * 128:(m + 1) * 128], AF.Identity, [p, mod], [hfT],
                                        scale=mod[:, Gb + c:Gb + c + 1], bias=mod[:, Bb + c:Bb + c + 1])
                                else:
                                    ts(DVE, o, p[:, m * 128:(m + 1) * 128], mod[:, Gb + c:Gb + c + 1], ALU.mult, [p, mod], [hfT],
                                       s2=mod[:, Bb + c:Bb + c + 1], op1=ALU.add)
                        tt(POOL, xt[:], xt[:], g1A[:], ALU.mult, [xt, g1A], [xt])
                        tt(POOL, xt[:], xt[:], b1A[:], ALU.add, [xt, b1A], [xt])
                    chk('b_hf')
                    for g in range(11):
                        wsl = wslb.get()
                        if first:
                            fw.dma(POOL, wsl[:].rearrange("p (c n) -> p c n", c=8),
                                   wup_d.rearrange("(c p) n -> p c n", p=128)[:, :, g * 512:(g + 1) * 512], wsl, [], [wsl])
                            fw.dma(SP, wu16_d[g], wsl[:], wsl, [wsl], [wu16_b[g]])
                        else:
                            fw.dma(SP, wsl[:], wu16_d[g], wsl, [wu16_b[g]], [wsl])
                        chk(f"b_g{g}_load")
                        for mi in range(4):
                            M = g * 4 + mi
                            p = ps()
                            mm(p, [(p[:, :], wsl[:, c * 512 + mi * 128:c * 512 + (mi + 1) * 128], hfT[:, c * 512:(c + 1) * 512], c == 0, c == 7) for c in range(8)], [wsl, hfT])
                            uc = uc_pool.get()
                            cp(POOL, uc[:, 0:2], carryf[:, M * 2:M * 2 + 2], [carryf], [uc])
                            cp(ACT, uc[:, 2:514], p[:, :], [p], [uc])
                            cp(POOL, carryf[:, M * 2:M * 2 + 2], uc[:, 512:514], [uc], [carryf])
                            acc = acc_pool.get()
                            wc = lambda k: pp[:, O_FC + k * 44 + M:O_FC + k * 44 + M + 1]
                            ts(DVE, acc[:], uc[:, 2:514], wc(2), ALU.mult, [uc, pp], [acc], s2=pp[:, O_FB + M:O_FB + M + 1], op1=ALU.add)
                            stt(DVE, acc[:], uc[:, 1:513], wc(1), acc[:], ALU.mult, ALU.add, [uc, pp, acc], [acc])
                            stt(DVE, acc[:], uc[:, 0:512], wc(0), acc[:], ALU.mult, ALU.add, [uc, pp, acc], [acc])
                            chk(f"b_m{M}_conv")
                            if M < 22:
                                act(aT[:, M * 512:(M + 1) * 512], acc[:], AF.Silu, [acc], [aT])
                            else:
                                Mg = M - 22
                                tt(DVE, aT[:, Mg * 512:(Mg + 1) * 512], aT[:, Mg * 512:(Mg + 1) * 512], acc[:], ALU.mult, [aT, acc], [aT])
                    dump("aT", aT, aT[:, 0:2048], [128, 2048], BF16)
                    chk("b_up")
                    for t4 in range(4):
                        ti = b * NT + s * 4 + t4
                        cs0 = t4 * 128
                        xt = x1t[t4]
                        for half in range(2):
                            p = ps()
                            mm(p, [(p[:, :], aT[:, c * 512 + cs0:c * 512 + cs0 + 128], wd[:, c * D + half * 512:c * D + (half + 1) * 512], c == 0, c == 21)
                                   for c in range(22)], [aT, wd])
                            tmp = tmp_pool.get()
                            tt(DVE, tmp[:], p[:, :], gt1[1][b][:, half * 512:(half + 1) * 512], ALU.mult, [p, gt1[1][b]], [tmp])
                            tt(POOL, xt[:, half * 512:(half + 1) * 512], xt[:, half * 512:(half + 1) * 512], tmp[:], ALU.add, [xt, tmp], [xt])
                        st = ln_stats(xt, xt)
                        act(xt[:], xt[:], AF.Identity, [xt, st], [xt], scale=st[:, 16:17], bias=st[:, 17:18])
                        tt(DVE, xt[:], xt[:], g2[:], ALU.mult, [xt, g2], [xt])
                        tt(POOL, xt[:], xt[:], b2[:], ALU.add, [xt, b2], [xt])
                        ob = Tl(None, f"out{ti}")
                        fw.dma(SP, out_d[b, (s * 4 + t4) * 128:(s * 4 + t4 + 1) * 128, :], xt[:], xt, [xt], [ob])
            pade = os.environ.get("PADE", "a")
            for _ in range(int(os.environ.get("PAD", "0"))):
                if "d" in pade:
                    fw.op(DVE, lambda h: h.memset(carryf[:, 0:2], 0.0), [], [carryf])
                if "p" in pade:
                    fw.op(POOL, lambda h: h.memset(carryf[:, 2:4], 0.0), [], [carryf])
                if "a" in pade:
                    act(carryf[:, 4:6], carryf[:, 0:2], AF.Copy, [carryf], [carryf])
            fw.finish()
    except _Stop:
        pass
    return nc, dbg_d


x1_bufs = {}
xr_bufs = {}


def _consts():
    i = np.arange(128)
    k = i[:, None]
    j = i[None, :]
    ident = (k == j).astype(np.float32)
    Uc = (k <= j).astype(np.float32)
    Lp = (k > j).astype(np.float32)
    ones = np.ones((128, 128), np.float32)
    same = (k // 64) == (j // 64)
    M1 = -((k > j) & same).astype(np.float32)
    M2 = ((k >= 64) & (j < 64)).astype(np.float32)
    MU = (k <= j).astype(np.float32)
    UN16 = -Uc / 16.0
    return np.concatenate([ident, Uc, Lp, ones, M1, M2, MU, UN16], axis=1).astype(np.float32)


def _prep(inputs, core, NB):
    f = lambda a: np.ascontiguousarray(np.asarray(a, dtype=np.float32))
    b0 = core * NB
    c = f(inputs["c"])[b0:b0 + NB]
    cT = c.reshape(NB, 8, 128).transpose(2, 1, 0).reshape(128, 8 * NB)
    fm = lambda v: f(v).reshape(-1, 128).T
    pp = np.zeros((128, NP_COLS), np.float32)
    pp[:, O_G0:O_G0 + 8] = fm(inputs["ln0_g"])
    pp[:, O_B0:O_B0 + 8] = fm(inputs["ln0_b"])
    pp[:, O_G1:O_G1 + 8] = fm(inputs["ln1_g"][0])
    pp[:, O_B1:O_B1 + 8] = fm(inputs["ln1_b"][0])
    pp[:, O_BADA:O_BADA + 48] = fm(inputs["b_ada"][0])
    dc = f(inputs["dn_conv"][0])
    for k in range(4):
        pp[:, O_DNC + k * 12:O_DNC + (k + 1) * 12] = fm(dc[k])
    pp[:, O_DNG] = f(inputs["dn_norm_g"][0])
    pp[:, O_GLG] = f(inputs["gla_norm_g"][0])
    fc = f(inputs["ffn_conv"][0])
    for k in range(3):
        pp[:, O_FC + k * 44:O_FC + (k + 1) * 44] = fm(fc[k])
    pp[:, O_FB:O_FB + 44] = fm(inputs["ffn_conv_b"][0])
    pp[:, O_NAL:O_NAL + 4] = f(inputs["dn_a_log"][0])[None, :]
    pp[:, O_DTB:O_DTB + 4] = f(inputs["dn_dt_bias"][0])[None, :]
    ba = f(inputs["b_ada"][0])
    vecs = np.stack([f(inputs["ln0_g"]), f(inputs["ln0_b"]), f(inputs["ln1_g"][0]), f(inputs["ln1_b"][0]),
                     f(inputs["ln2_g"][0]), f(inputs["ln2_b"][0]), ba[2048:3072], ba[5120:6144]], axis=0)
    w2b = np.concatenate([f(inputs["gla_w_gate2"][0]), f(inputs["gla_b_gate"][0])[None, :]], axis=0)
    return {
        "x": f(inputs["x"])[b0:b0 + NB],
        "cT": np.ascontiguousarray(cT),
        "w_ada": f(inputs["w_ada"][0]),
        "w_in": f(inputs["w_in"][0]),
        "w_o": f(inputs["w_o"][0]),
        "w_up": f(inputs["ffn_w_up"][0]),
        "w_down": f(inputs["ffn_w_down"][0]),
        "pp": pp,
        "vecs": np.ascontiguousarray(vecs),
        "w2b": np.ascontiguousarray(w2b),
        "cst": _consts(),
    }


def run(inputs, n_cores=8, dbg=None, stop_at=None):
    x = np.asarray(inputs["x"])
    B, SEQ, _ = x.shape
    NB = B // n_cores
    x1_bufs.clear()
    xr_bufs.clear()
    nc, dbg_d = build(SEQ, NB, dbg, stop_at)
    in_maps = [_prep(inputs, core, NB) for core in range(n_cores)]
    res = run_bass_kernel_spmd(nc, in_maps, core_ids=list(range(n_cores)))
    out = np.concatenate([np.asarray(r["out"]) for r in res.results], axis=0).astype(np.float32)
    return out, res, dbg_d


def kernel(**inputs):
    out, _, _ = run(inputs, 8)
    return out
```

```python
import contextlib
import os
import numpy as np
import concourse.bass as bass
import concourse.mybir as mybir
from concourse.bass_utils import run_bass_kernel_spmd

F32 = mybir.dt.float32
BF16 = mybir.dt.bfloat16
AF = mybir.ActivationFunctionType
ALU = mybir.AluOpType

D = 1024
ALPHA = 2.0 ** 0.25
EPS = 1e-6
NP_COLS = 320
O_G0, O_B0, O_G1, O_B1, O_BADA, O_DNC, O_DNG, O_GLG, O_FC, O_FB, O_NAL, O_DTB = 0, 8, 16, 24, 32, 80, 128, 129, 130, 262, 306, 310
WI_GROUPS = [(0, 512), (512, 512), (1024, 512), (1536, 512), (2048, 520), (2568, 512), (3080, 528)]
GW = 528
SAFE_SAME_ENGINE = True


class Buf:
    __slots__ = ("name", "w", "r", "dsem", "dcnt")

    def __init__(self, name):
        self.name = name
        self.w = None
        self.r = {}
        self.dsem = None
        self.dcnt = 0


class Tl:
    def __init__(self, t, name, psum=False):
        self.t = t
        self.b = Buf(name)
        self.psum = psum

    def __getitem__(self, k):
        return self.t[k]


class Eng:
    def __init__(self, h, sem, name):
        self.h = h
        self.sem = sem
        self.cnt = 0
        self.known = {}
        self.name = name


class FW:
    def __init__(self, nc, es):
        self.nc = nc
        self.es = es
        self.nsem = 0
        self.PE = Eng(nc.tensor, self.newsem("pe"), "pe")
        self.ACT = Eng(nc.scalar, self.newsem("act"), "act")
        self.DVE = Eng(nc.vector, self.newsem("dve"), "dve")
        self.POOL = Eng(nc.gpsimd, self.newsem("pool"), "pool")
        self.SP = Eng(nc.sync, self.newsem("sp"), "sp")
        self.engs = [self.PE, self.ACT, self.DVE, self.POOL, self.SP]
        self.dsems = []

    def newsem(self, name):
        self.nsem += 1
        return self.es.enter_context(self.nc.semaphore(f"s{self.nsem}_{name}"))

    def _wait(self, eng, evs):
        best = {}
        for ev in evs:
            if ev is None:
                continue
            sem, val = ev
            k = id(sem)
            if sem is eng.sem and (eng is self.PE or not SAFE_SAME_ENGINE):
                continue
            if eng.known.get(k, 0) >= val:
                continue
            if k not in best or best[k][1] < val:
                best[k] = (sem, val)
        for sem, val in best.values():
            eng.h.wait_ge(sem, val)
            eng.known[id(sem)] = val

    def _deps(self, R, W, skip_sem=None):
        evs = []
        for t in R:
            evs.append(t.b.w)
            if t.psum:
                evs.extend(t.b.r.values())
        for t in W:
            if not (skip_sem is not None and t.b.w is not None and t.b.w[0] is skip_sem):
                evs.append(t.b.w)
            evs.extend(t.b.r.values())
        return evs

    def _commit(self, ev, R, W):
        k = id(ev[0])
        for t in R:
            old = t.b.r.get(k)
            if old is None or old[1] < ev[1]:
                t.b.r[k] = ev
        for t in W:
            t.b.w = ev
            t.b.r = {}

    def op(self, eng, fn, R, W, sig=True):
        self._wait(eng, self._deps(R, W))
        ins = fn(eng.h)
        if sig:
            eng.cnt += 1
            ins.then_inc(eng.sem, 1)
            self._commit((eng.sem, eng.cnt), R, W)
        return ins

    def group(self, eng, fns, R, W):
        self._wait(eng, self._deps(R, W))
        ins = None
        for fn in fns:
            ins = fn(eng.h)
        eng.cnt += 1
        ins.then_inc(eng.sem, 1)
        self._commit((eng.sem, eng.cnt), R, W)

    def dma(self, eng, out, in_, sbt, R, W, **kw):
        b = sbt.b
        if b.dsem is None:
            b.dsem = self.newsem("d_" + b.name)
            self.dsems.append(b)
        self._wait(eng, self._deps(R, W, skip_sem=b.dsem))
        ins = eng.h.dma_start(out=out, in_=in_, **kw)
        b.dcnt += 16
        ins.then_inc(b.dsem, 16)
        self._commit((b.dsem, b.dcnt), R, W)

    def finish(self):
        for e in self.engs:
            evs = [(o.sem, o.cnt) for o in self.engs if o is not e and o.cnt > 0]
            evs += [(b.dsem, b.dcnt) for b in self.dsems]
            self._wait(e, evs)


class _Stop(Exception):
    pass


def build(SEQ, NB=2, dbg=None, stop_at=None):
    nc = bass.Bass("TRN2", target_bir_lowering=False)
    NT = SEQ // 128
    NS = SEQ // 512
    dr = lambda n, s, dt=F32, kind="ExternalInput": nc.dram_tensor(n, s, dt, kind=kind).ap()
    x_d = dr("x", [NB, SEQ, D])
    cT_d = dr("cT", [128, 8 * NB])
    wada_d = dr("w_ada", [D, 6 * D])
    win_d = dr("w_in", [D, 3608])
    wo_d = dr("w_o", [D, D])
    wup_d = dr("w_up", [D, 5632])
    wdn_d = dr("w_down", [2816, D])
    pp_d = dr("pp", [128, NP_COLS])
    vec_d = dr("vecs", [8, D])
    w2b_d = dr("w2b", [17, 256])
    cst_d = dr("cst", [128, 8 * 128])
    out_d = dr("out", [NB, SEQ, D], kind="ExternalOutput")
    wi16_d = dr("wi16", [7, 128, 8 * GW], BF16, kind="Internal")
    wu16_d = dr("wu16", [11, 128, 8 * 512], BF16, kind="Internal")
    xr_d = dr("xr_s", [NB * NT, 128, D], F32, kind="Internal")
    x1_d = dr("x1_s", [NB * NT, 128, D], F32, kind="Internal")
    dbg_d = {}

    es = contextlib.ExitStack()
    try:
      with es:
        fw = FW(nc, es)
        def chk(name):
            if stop_at == name:
                fw.finish()
                raise _Stop()
        PE, ACT, DVE, POOL, SP = fw.PE, fw.ACT, fw.DVE, fw.POOL, fw.SP
        cnt = [0]

        cur = [es]

        def sb(shape, dt, name=None):
            cnt[0] += 1
            name = name or f"t{cnt[0]}"
            return Tl(cur[0].enter_context(nc.sbuf_tensor("sb_" + name, shape, dt)), name)

        psb = [Tl(es.enter_context(nc.psum_tensor(f"ps{i}", [128, 512], F32)), f"ps{i}", psum=True) for i in range(8)]
        psi = [0]

        def ps():
            psi[0] = (psi[0] + 1) % 8
            return psb[psi[0]]

        class Pool_:
            def __init__(self, n, shape, dt, nm):
                self.l = [sb(shape, dt, f"{nm}{i}") for i in range(n)]
                self.i = 0

            def get(self):
                self.i = (self.i + 1) % len(self.l)
                return self.l[self.i]

        def act(out, in_, func, R, W, **kw):
            fw.op(ACT, lambda h: h.activation(out=out, in_=in_, func=func, **kw), R, W)

        def tt(E, out, in0, in1, op, R, W):
            fw.op(E, lambda h: h.tensor_tensor(out=out, in0=in0, in1=in1, op=op), R, W)

        def ts(E, out, in0, s1, op0, R, W, s2=None, op1=None):
            if op1 is None:
                fw.op(E, lambda h: h.tensor_scalar(out=out, in0=in0, scalar1=s1, scalar2=None, op0=op0), R, W)
            else:
                fw.op(E, lambda h: h.tensor_scalar(out=out, in0=in0, scalar1=s1, scalar2=s2, op0=op0, op1=op1), R, W)

        def stt(E, out, in0, s, in1, op0, op1, R, W):
            fw.op(E, lambda h: h.scalar_tensor_tensor(out=out, in0=in0, scalar=s, in1=in1, op0=op0, op1=op1), R, W)

        def cp(E, out, in_, R, W):
            if E is ACT:
                act(out, in_, AF.Copy, R, W)
            else:
                fw.op(E, lambda h: h.tensor_copy(out=out, in_=in_), R, W)

        def mm(pst, specs, R):
            fns = [(lambda h, o=o, l=l, r=r, s=s, e=e: h.matmul(o, lhsT=l, rhs=r, start=s, stop=e)) for (o, l, r, s, e) in specs]
            fw.group(PE, fns, R, [pst])

        def tr(pst, specs, R):
            fns = [(lambda h, o=o, i=i, d=d: h.transpose(out=o, in_=i, identity=d)) for (o, i, d) in specs]
            fw.group(PE, fns, R, [pst])

        def dump(name, tl, ap, shape, dt=F32):
            if dbg is None or name not in dbg:
                return
            key = name
            n = 0
            while key in dbg_d:
                n += 1
                key = f"{name}.{n}"
            d = nc.dram_tensor("dbg_" + key.replace(".", "_"), shape, dt, kind="ExternalOutput").ap()
            dbg_d[key] = "dbg_" + key.replace(".", "_")
            fw.dma(SP, d, ap, tl, [tl], [])

        cst = sb([128, 1024], F32, "cst")
        fw.dma(SP, cst[:], cst_d[:, :], cst, [], [cst])
        IDF, U, LP, ONES, M1, M2, MU, UN16 = [cst[:, i * 128:(i + 1) * 128] for i in range(8)]
        pp = sb([128, NP_COLS], F32, "pp")
        fw.dma(SP, pp[:], pp_d[:, :], pp, [], [pp])
        w2b = sb([17, 256], F32, "w2b")
        fw.dma(SP, w2b[:], w2b_d[:, :], w2b, [], [w2b])
        cb = sb([128, 256], BF16, "cb")
        cp(DVE, cb[:, 0:128], IDF, [cst], [cb])
        cp(DVE, cb[:, 128:256], ONES, [cst], [cb])
        IDB, ONESB = cb[:, 0:128], cb[:, 128:256]
        sm = sb([128, 64], F32, "sm")
        fw.op(DVE, lambda h: h.memset(sm[:, 4:5], EPS), [], [sm])
        fw.op(DVE, lambda h: h.memset(sm[:, 5:6], 1.0), [], [sm])
        act(sm[:, 0:4], pp[:, O_NAL:O_NAL + 4], AF.Exp, [pp], [sm])
        ts(DVE, sm[:, 0:4], sm[:, 0:4], -1.0, ALU.mult, [sm], [sm])
        NEA, EPSC, ONEC = sm[:, 0:4], sm[:, 4:5], sm[:, 5:6]

        def bcast(row, name):
            t = sb([128, D], F32, name)
            fw.dma(SP, t[:], vec_d[row, :].partition_broadcast(128), t, [], [t])
            return t

        gt1 = [[sb([128, D], F32, f"gt1_{w}{b}") for b in range(NB)] for w in range(2)]
        mod = sb([128, 64], F32, "mod")
        GB = {}
        es_s = contextlib.ExitStack()
        with es_s:
            cur[0] = es_s
            cT = sb([128, 8 * NB], F32, "cT")
            fw.dma(SP, cT[:], cT_d[:, :], cT, [], [cT])
            cond = sb([128, 8 * NB], F32, "cond")
            act(cond[:], cT[:], AF.Silu, [cT], [cond])
            crep = sb([128, 8 * NB * 128], F32, "crep")
            for j in range(8 * NB):
                act(crep[:, j * 128:(j + 1) * 128], ONES, AF.Identity, [cst, cond], [crep], scale=cond[:, j:j + 1])
            modT = sb([128, 48 * NB], F32, "modT")
            bgt = [bcast(6, "bgta"), bcast(7, "bgtf")]
            wa_pool = Pool_(2, [128, 8 * 512], F32, "wa")
            for blk in range(12):
                wa = wa_pool.get()
                fw.dma(SP, wa[:].rearrange("p (c n) -> p c n", c=8),
                       wada_d.rearrange("(c p) n -> p c n", p=128)[:, :, blk * 512:(blk + 1) * 512], wa, [], [wa])
                sec = blk // 2
                if sec in (2, 5):
                    w = 0 if sec == 2 else 1
                    half = blk % 2
                    for b in range(NB):
                        p = ps()
                        mm(p, [(p[:, :], crep[:, (c * NB + b) * 128:(c * NB + b + 1) * 128], wa[:, c * 512:(c + 1) * 512], c == 0, c == 7)
                               for c in range(8)], [crep, wa])
                        stt(DVE, gt1[w][b][:, half * 512:(half + 1) * 512], p[:, :], 1.0, bgt[w][:, half * 512:(half + 1) * 512],
                            ALU.add, ALU.add, [p, bgt[w]], [gt1[w][b]])
                else:
                    p = ps()
                    specs = []
                    for m in range(4):
                        for c in range(8):
                            specs.append((p[:, m * NB:(m + 1) * NB], wa[:, c * 512 + m * 128: c * 512 + (m + 1) * 128],
                                          cond[:, c * NB:(c + 1) * NB], c == 0, c == 7))
                    mm(p, specs, [cond, wa])
                    j0 = blk * 4
                    for m in range(4):
                        ts(DVE, modT[:, (j0 + m) * NB:(j0 + m + 1) * NB], p[:, m * NB:(m + 1) * NB], pp[:, O_BADA + j0 + m:O_BADA + j0 + m + 1],
                           ALU.add, [p, pp], [modT])
            def modcol(j0, b):
                return modT[:].rearrange("p (j b) -> p b j", b=NB)[:, b, j0:j0 + 8]

            for w, (jsh, jsc, og, ob) in enumerate([(0, 8, O_G0, O_B0), (24, 32, O_G1, O_B1)]):
                for b in range(NB):
                    base = ((w * NB + b) * 2) * 8
                    Gc = mod[:, base:base + 8]
                    Bc = mod[:, base + 8:base + 16]
                    ts(DVE, Gc, modcol(jsc, b), 1.0, ALU.add, [modT], [mod])
                    tt(DVE, Bc, Gc, pp[:, ob:ob + 8], ALU.mult, [mod, pp], [mod])
                    tt(DVE, Bc, Bc, modcol(jsh, b), ALU.add, [mod, modT], [mod])
                    tt(DVE, Gc, Gc, pp[:, og:og + 8], ALU.mult, [mod, pp], [mod])
                    GB[(w, b)] = (base, base + 8)

            fw.finish()
            cur[0] = es
        chk('setup')

        stat = Pool_(4, [128, 32], F32, "stat")

        def ln_stats(zt, zap):
            st = stat.get()
            fw.op(DVE, lambda h: h.bn_stats(out=st[:, 0:6], in_=zap[:, 0:512]), [zt], [st])
            fw.op(DVE, lambda h: h.bn_stats(out=st[:, 6:12], in_=zap[:, 512:1024]), [zt], [st])
            fw.op(DVE, lambda h: h.bn_aggr(out=st[:, 12:14], in_=st[:, 0:12].rearrange("p (a b) -> p a b", a=2)), [st], [st])
            act(st[:, 14:15], st[:, 13:14], AF.Sqrt, [st, sm], [st], bias=EPSC, scale=1.0)
            fw.op(DVE, lambda h: h.reciprocal(out=st[:, 16:17], in_=st[:, 14:15]), [st], [st])
            stt(DVE, st[:, 17:18], st[:, 12:13], -1.0, st[:, 16:17], ALU.mult, ALU.mult, [st], [st])
            return st

        es_a = contextlib.ExitStack()
        _outer_es = es
        with es_a:
            def sba(shape, dt, name):
                return Tl(es_a.enter_context(nc.sbuf_tensor("sa_" + name, shape, dt)), name)

            class PoolA:
                def __init__(self, n, shape, dt, nm):
                    self.l = [sba(shape, dt, f"{nm}{i}") for i in range(n)]
                    self.i = 0

                def get(self):
                    self.i = (self.i + 1) % len(self.l)
                    return self.l[self.i]

            gA = sba([128, D], F32, "gA")
            bA = sba([128, D], F32, "bA")
            fw.dma(SP, gA[:], vec_d[0, :].partition_broadcast(128), gA, [], [gA])
            fw.dma(SP, bA[:], vec_d[1, :].partition_broadcast(128), bA, [], [bA])
            ts(DVE, gA[:], gA[:], ALPHA, ALU.mult, [gA], [gA])
            ts(DVE, bA[:], bA[:], ALPHA, ALU.mult, [bA], [bA])
            wo = sba([128, 8 * D], BF16, "wo")
            fw.dma(POOL, wo[:].rearrange("p (c n) -> p c n", c=8), wo_d.rearrange("(c p) n -> p c n", p=128), wo, [], [wo])
            wslot = PoolA(2, [128, 8 * GW], BF16, "wsl")
            wi16_b = [Tl(None, f"wi16_{g}") for g in range(7)]
            xt_pool = PoolA(3, [128, D], F32, "xt")
            xpre = {}

            def load_x(b_, s_, t4_):
                if b_ >= NB or (b_, s_, t4_) in xpre:
                    return
                xt_ = xt_pool.get()
                fw.dma(SP, xt_[:], x_d[b_, (s_ * 4 + t4_) * 128:(s_ * 4 + t4_ + 1) * 128, :], xt_, [], [xt_])
                xpre[(b_, s_, t4_)] = xt_

            hT = sba([128, 8 * 512], BF16, "hT")
            qkvT = sba([128, 12 * 512], BF16, "qkvT")
            zgT = sba([128, 8 * 512], BF16, "zgT")
            glqkT = sba([64, 8 * 512], BF16, "glqkT")
            glrT = sba([17, 512], F32, "glrT")
            glv = sba([128, 4 * 512], BF16, "glv")
            glkt = sba([128, 4 * 256], F32, "glkt")
            abt = sba([128, 32], F32, "abt")
            carry = sba([128, 12 * 3], F32, "carry")
            xc_pool = PoolA(2, [128, 515], F32, "xc")
            s32 = PoolA(3, [128, 512], F32, "s32_")
            s16 = PoolA(2, [128, 512], BF16, "s16_")

            class Ctx:
                def __init__(self, banks, n32, n16, tag):
                    self.banks = banks
                    self.bi = 0
                    self.p32 = PoolA(n32, [128, 512], F32, tag + "f") if n32 else None
                    self.p16 = PoolA(n16, [128, 512], BF16, tag + "h") if n16 else None

                def ps(self):
                    self.bi = (self.bi + 1) % len(self.banks)
                    return self.banks[self.bi]

            cpre = Ctx(psb[0:3], 6, 7, "cp")
            crec = Ctx(psb[3:5], 0, 0, "cr")
            cgla = Ctx(psb[5:7], 4, 4, "cg")
            cpost = Ctx(psb[7:8], 2, 2, "co")
            pset = []
            for i in range(2):
                dct = {n: sba([128, 512], BF16, f"d{i}_{n}") for n in ["Aoff", "At", "qdT", "TT", "kdec"]}
                dct["vb"] = sba([128, 512], F32, f"d{i}_vb")
                dct["g"] = sba([128, 64], F32, f"d{i}_g")
                pset.append(dct)
            vnew = sba([128, 512], BF16, "vnew")
            r_t = sba([128, 512], BF16, "r_t")
            S = sba([128, 512], F32, "S")
            Sb = sba([128, 512], BF16, "Sb")
            S2 = sba([64, 512], F32, "S2")
            S2b = sba([64, 512], BF16, "S2b")
            oT_pool = PoolA(2, [128, 1024], F32, "oT")
            z_pool = PoolA(1, [128, D], F32, "z")
            fw.op(POOL, lambda h: h.memset(glrT[:, :], 1.0), [], [glrT])

            for b in range(NB):
                Gb, Bb = GB[(0, b)]
                fw.op(POOL, lambda h: h.memset(S[:], 0.0), [], [S])
                fw.op(POOL, lambda h: h.memset(Sb[:], 0.0), [], [Sb])
                fw.op(POOL, lambda h: h.memset(S2[:], 0.0), [], [S2])
                fw.op(POOL, lambda h: h.memset(S2b[:], 0.0), [], [S2b])
                fw.op(POOL, lambda h: h.memset(carry[:], 0.0), [], [carry])
                for s in range(NS):
                    first = (b == 0 and s == 0)
                    for t4 in range(4):
                        ti = b * NT + s * 4 + t4
                        load_x(b, s, t4)
                        xt = xpre.pop((b, s, t4))
                        if t4 < 3:
                            load_x(b, s, t4 + 1)
                        st = ln_stats(xt, xt)
                        act(xt[:], xt[:], AF.Identity, [xt, st], [xt], scale=st[:, 16:17], bias=st[:, 17:18])
                        if t4 == 0:
                            dump("xn", xt, xt[:], [128, D])
                        for half in range(2):
                            p = ps()
                            tr(p, [(p[:, m * 128:(m + 1) * 128], xt[:, (half * 4 + m) * 128:(half * 4 + m + 1) * 128], IDF) for m in range(4)], [xt, cst])
                            for m in range(4):
                                c = half * 4 + m
                                E = ACT if half == 0 else DVE
                                o = hT[:, c * 512 + t4 * 128: c * 512 + (t4 + 1) * 128]
                                if E is ACT:
                                    act(o, p[:, m * 128:(m + 1) * 128], AF.Identity, [p, mod], [hT],
                                        scale=mod[:, Gb + c:Gb + c + 1], bias=mod[:, Bb + c:Bb + c + 1])
                                else:
                                    ts(DVE, o, p[:, m * 128:(m + 1) * 128], mod[:, Gb + c:Gb + c + 1], ALU.mult, [p, mod], [hT],
                                       s2=mod[:, Bb + c:Bb + c + 1], op1=ALU.add)
                        tt(POOL, xt[:], xt[:], gA[:], ALU.mult, [xt, gA], [xt])
                        tt(POOL, xt[:], xt[:], bA[:], ALU.add, [xt, bA], [xt])
                        xrb = Tl(None, f"xr{ti}")
                        xr_bufs[ti] = xrb
                        fw.dma(SP, xr_d[ti], xt[:], xt, [xt], [xrb])
                    dump("hT", hT, hT[:], [128, 4096], BF16)
                    chk("ln0")

                    def load_group(g):
                        c0, ncol = WI_GROUPS[g]
                        wsl = wslot.get()
                        w3 = wsl[:].rearrange("p (c n) -> p c n", c=8)
                        if first:
                            fw.dma(POOL, w3[:, :, 0:ncol], win_d.rearrange("(c p) n -> p c n", p=128)[:, :, c0:c0 + ncol], wsl, [], [wsl])
                            fw.dma(SP, wi16_d[g], wsl[:], wsl, [wsl], [wi16_b[g]])
                        else:
                            fw.dma(SP, wsl[:], wi16_d[g], wsl, [wi16_b[g]], [wsl])
                        return wsl

                    def fm_proj(wsl, col, M):
                        p = ps()
                        mm(p, [(p[0:M, :], wsl[:, c * GW + col: c * GW + col + M], hT[:, c * 512:(c + 1) * 512], c == 0, c == 7) for c in range(8)], [wsl, hT])
                        return p

                    def tm_proj(wsl, col, N, t4):
                        p = ps()
                        mm(p, [(p[:, 0:N], hT[:, c * 512 + t4 * 128: c * 512 + (t4 + 1) * 128], wsl[:, c * GW + col: c * GW + col + N], c == 0, c == 7) for c in range(8)], [wsl, hT])
                        return p

                    for g in range(3):
                        wsl = load_group(g)
                        for hh in range(4):
                            m = g * 4 + hh
                            p = fm_proj(wsl, hh * 128, 128)
                            xc = xc_pool.get()
                            cp(POOL, xc[:, 0:3], carry[:, m * 3:(m + 1) * 3], [carry], [xc])
                            cp(ACT, xc[:, 3:515], p[:, :], [p], [xc])
                            cp(POOL, carry[:, m * 3:(m + 1) * 3], xc[:, 512:515], [xc], [carry])
                            acc = s32.get()
                            wcol = lambda k: pp[:, O_DNC + k * 12 + m: O_DNC + k * 12 + m + 1]
                            act(acc[:], xc[:, 3:515], AF.Identity, [xc, pp], [acc], scale=wcol(3))
                            stt(DVE, acc[:], xc[:, 2:514], wcol(2), acc[:], ALU.mult, ALU.add, [xc, pp, acc], [acc])
                            stt(DVE, acc[:], xc[:, 1:513], wcol(1), acc[:], ALU.mult, ALU.add, [xc, pp, acc], [acc])
                            stt(DVE, acc[:], xc[:, 0:512], wcol(0), acc[:], ALU.mult, ALU.add, [xc, pp, acc], [acc])
                            dst = qkvT[:, m * 512:(m + 1) * 512]
                            if g == 2:
                                act(dst, acc[:], AF.Silu, [acc], [qkvT])
                            else:
                                act(acc[:], acc[:], AF.Silu, [acc], [acc])
                                sq = s16.get()
                                act(sq[:], acc[:], AF.Square, [acc], [sq])
                                p2 = ps()
                                mm(p2, [(p2[:, :], ONESB, sq[:], True, True)], [cb, sq])
                                rn = s32.get()
                                act(rn[:], p2[:, :], AF.Ln, [p2, sm], [rn], bias=EPSC, scale=1.0)
                                act(rn[:], rn[:], AF.Exp, [rn], [rn], scale=-0.5)
                                if g == 0:
                                    stt(DVE, dst, acc[:], 128.0 ** -0.5, rn[:], ALU.mult, ALU.mult, [acc, rn], [qkvT])
                                else:
                                    tt(DVE, dst, acc[:], rn[:], ALU.mult, [acc, rn], [qkvT])
                    dump("qkvT", qkvT, qkvT[:], [128, 12 * 512], BF16)
                    chk("qkv")
                    wsl = load_group(3)
                    for hh in range(4):
                        p = fm_proj(wsl, hh * 128, 128)
                        tmp = s32.get()
                        act(tmp[:], p[:, :], AF.Silu, [p], [tmp])
                        ts(DVE, zgT[:, hh * 512:(hh + 1) * 512], tmp[:], pp[:, O_DNG:O_DNG + 1], ALU.mult, [tmp, pp], [zgT])
                    wsl = load_group(4)
                    for t4 in range(4):
                        p = tm_proj(wsl, 0, 8, t4)
                        cp(DVE, abt[:, t4 * 8:(t4 + 1) * 8], p[:, 0:8], [p], [abt])
                        p = tm_proj(wsl, 8 + 256, 256, t4)
                        cp(ACT, glkt[:, t4 * 256:(t4 + 1) * 256], p[:, 0:256], [p], [glkt])
                    for hq in range(8):
                        p = fm_proj(wsl, 8 + hq * 64, 64)
                        cp(ACT if hq % 2 else DVE, glqkT[:, hq * 512:(hq + 1) * 512], p[0:64, :], [p], [glqkT])
                    wsl = load_group(5)
                    for t4 in range(4):
                        p = tm_proj(wsl, 0, 512, t4)
                        cp(ACT if t4 % 2 else DVE, glv[:, t4 * 512:(t4 + 1) * 512], p[:, :], [p], [glv])
                    wsl = load_group(6)
                    for hh in range(4):
                        p = fm_proj(wsl, hh * 128, 128)
                        tmp = s32.get()
                        act(tmp[:], p[:, :], AF.Silu, [p], [tmp])
                        ts(DVE, zgT[:, (4 + hh) * 512:(5 + hh) * 512], tmp[:], pp[:, O_GLG:O_GLG + 1], ALU.mult, [tmp, pp], [zgT])
                    p = fm_proj(wsl, 512, 16)
                    cp(DVE, glrT[0:16, :], p[0:16, :], [p], [glrT])
                    dump("glv", glv, glv[:], [128, 2048], BF16)
                    dump("glqkT", glqkT, glqkT[:], [64, 4096], BF16)
                    chk("inproj")

                    H = lambda t, h: t[:, h * 128:(h + 1) * 128]

                    def hsl(base, h, cs0):
                        return slice(base + h * 512 + cs0, base + h * 512 + cs0 + 128)

                    def gen_pre(t4, P, C):
                        cs0 = t4 * 128
                        qTh = lambda h: qkvT[:, hsl(0, h, cs0)]
                        kTh = lambda h: qkvT[:, hsl(4 * 512, h, cs0)]
                        vTh = lambda h: qkvT[:, hsl(8 * 512, h, cs0)]
                        g = P["g"]
                        tt(DVE, g[:, 8:12], abt[:, t4 * 8:t4 * 8 + 4], pp[:, O_DTB:O_DTB + 4], ALU.add, [abt, pp], [g])
                        act(g[:, 8:12], g[:, 8:12], AF.Exp, [g], [g])
                        act(g[:, 8:12], g[:, 8:12], AF.Ln, [g, sm], [g], bias=ONEC, scale=1.0)
                        tt(DVE, g[:, 0:4], g[:, 8:12], NEA, ALU.mult, [g, sm], [g])
                        act(g[:, 12:16], abt[:, t4 * 8 + 4:t4 * 8 + 8], AF.Exp, [abt], [g], scale=-1.0)
                        ts(DVE, g[:, 12:16], g[:, 12:16], 1.0, ALU.add, [g], [g])
                        fw.op(DVE, lambda h: h.reciprocal(out=g[:, 4:8], in_=g[:, 12:16]), [g], [g])
                        LA, BETA = g[:, 0:4], g[:, 4:8]
                        yield
                        p = C.ps()
                        mm(p, [(p[:, 0:4], U, LA, True, True), (p[:, 4:8], LP, LA, True, True), (p[:, 8:12], ONES, LA, True, True)], [cst, g])
                        act(g[:, 16:28], p[:, 0:12], AF.Exp, [p], [g])
                        EG, EGR, GL = g[:, 16:20], g[:, 20:24], g[:, 24:28]
                        stt(DVE, g[:, 28:32], EG, -1.0, BETA, ALU.mult, ALU.mult, [g], [g])
                        ula = C.p32.get()
                        for h in range(4):
                            act(H(ula, h), U, AF.Identity, [cst, g], [ula], scale=LA[:, h:h + 1])
                        yield
                        pD = C.ps()
                        mm(pD, [(H(pD, h), H(ula, h), LP, True, True) for h in range(4)], [ula, cst])
                        pDt = C.ps()
                        mm(pDt, [(pDt[:, :], LP, ula[:], True, True)], [ula, cst])
                        pGr = C.ps()
                        mm(pGr, [(pGr[:, :], ONES, ula[:], True, True)], [ula, cst])
                        E = C.p32.get()
                        act(E[:], pD[:, :], AF.Exp, [pD], [E])
                        Et = C.p32.get()
                        act(Et[:], pDt[:, :], AF.Exp, [pDt], [Et])
                        eGrow = C.p32.get()
                        act(eGrow[:], pGr[:, :], AF.Exp, [pGr], [eGrow])
                        yield
                        F1 = C.p32.get()
                        F2 = C.p32.get()
                        for h in range(4):
                            stt(DVE, H(F1, h), H(E, h), BETA[:, h:h + 1], M1, ALU.mult, ALU.mult, [E, g, cst], [F1])
                            stt(DVE, H(F2, h), H(E, h), BETA[:, h:h + 1], M2, ALU.mult, ALU.mult, [E, g, cst], [F2])
                            tt(POOL, H(Et, h), H(Et, h), MU, ALU.mult, [Et, cst], [Et])
                        yield
                        pKK = C.ps()
                        mm(pKK, [(H(pKK, h), kTh(h), kTh(h), True, True) for h in range(4)], [qkvT])
                        pQK = C.ps()
                        mm(pQK, [(H(pQK, h), kTh(h), qTh(h), True, True) for h in range(4)], [qkvT])
                        X = C.p16.get()
                        tt(DVE, X[:], pKK[:, :], F1[:], ALU.mult, [pKK, F1], [X])
                        Aoff = P["Aoff"]
                        tt(DVE, Aoff[:], pKK[:, :], F2[:], ALU.mult, [pKK, F2], [Aoff])
                        At = P["At"]
                        tt(DVE, At[:], pQK[:, :], Et[:], ALU.mult, [pQK, Et], [At])
                        qdT = P["qdT"]
                        for h in range(4):
                            tt(POOL, H(qdT, h), qTh(h), H(eGrow, h), ALU.mult, [qkvT, eGrow], [qdT])
                        yield
                        pT = C.ps()
                        pTb = pT[:].bitcast(BF16)
                        tr(pT, [(pTb[:, h * 128:(h + 1) * 128], H(X, h), IDB) for h in range(4)], [X, cb])
                        XT = C.p16.get()
                        cp(ACT, XT[:], pTb[:, 0:512], [pT], [XT])
                        PT = C.p16.get()
                        for h in range(4):
                            tt(DVE, H(PT, h), H(XT, h), IDB, ALU.add, [XT, cb], [PT])
                        yield
                        pk = C.ps()
                        pkb = pk[:].bitcast(BF16)
                        tr(pk, [(pkb[:, h * 128:(h + 1) * 128], kTh(h), IDB) for h in range(4)]
                           + [(pkb[:, 512 + h * 128:512 + (h + 1) * 128], vTh(h), IDB) for h in range(4)], [qkvT, cb])
                        kdec = P["kdec"]
                        vb = P["vb"]
                        for h in range(4):
                            ts(DVE, H(kdec, h), pkb[:, h * 128:(h + 1) * 128], EGR[:, h:h + 1], ALU.mult, [pk, g], [kdec])
                            ts(DVE, H(vb, h), pkb[:, 512 + h * 128:512 + (h + 1) * 128], BETA[:, h:h + 1], ALU.mult, [pk, g], [vb])
                        yield
                        for lvl in range(1, 6):
                            pX = C.ps()
                            mm(pX, [(H(pX, h), H(XT, h), H(X, h), True, True) for h in range(4)], [X, XT])
                            if lvl < 5:
                                pXT = C.ps()
                                mm(pXT, [(H(pXT, h), H(X, h), H(XT, h), True, True) for h in range(4)], [X, XT])
                            Xn = C.p16.get()
                            cp(ACT, Xn[:], pX[:, :], [pX], [Xn])
                            if lvl < 5:
                                XTn = C.p16.get()
                                cp(DVE, XTn[:], pXT[:, :], [pXT], [XTn])
                            yield
                            pP = C.ps()
                            specs = []
                            for h in range(4):
                                specs.append((H(pP, h), IDB, H(PT, h), True, False))
                                specs.append((H(pP, h), H(Xn, h), H(PT, h), False, True))
                            mm(pP, specs, [cb, PT, Xn])
                            PTn = C.p16.get()
                            cp(ACT if lvl % 2 else DVE, PTn[:], pP[:, :], [pP], [PTn])
                            X, PT = Xn, PTn
                            if lvl < 5:
                                XT = XTn
                            yield
                        TbdT = PT
                        pT2 = C.ps()
                        pT2b = pT2[:].bitcast(BF16)
                        tr(pT2, [(pT2b[:, h * 128:(h + 1) * 128], H(TbdT, h), IDB) for h in range(4)], [TbdT, cb])
                        pZ = C.ps()
                        mm(pZ, [(H(pZ, h), H(P["Aoff"], h), H(TbdT, h), True, True) for h in range(4)], [P["Aoff"], TbdT])
                        Tbd = C.p16.get()
                        cp(ACT, Tbd[:], pT2b[:, 0:512], [pT2], [Tbd])
                        nZ = C.p16.get()
                        ts(DVE, nZ[:], pZ[:, :], -1.0, ALU.mult, [pZ], [nZ])
                        yield
                        pTT = C.ps()
                        specs = []
                        for h in range(4):
                            specs.append((H(pTT, h), IDB, H(TbdT, h), True, False))
                            specs.append((H(pTT, h), H(Tbd, h), H(nZ, h), False, True))
                        mm(pTT, specs, [cb, TbdT, Tbd, nZ])
                        TT = P["TT"]
                        cp(ACT, TT[:], pTT[:, :], [pTT], [TT])
                        yield

                    def gen_rec(t4, P, C, oT):
                        cs0 = t4 * 128
                        kTh = lambda h: qkvT[:, hsl(4 * 512, h, cs0)]
                        g = P["g"]
                        GL, NEBG = g[:, 24:28], g[:, 28:32]
                        vb, TT, At, qdT, kdec = P["vb"], P["TT"], P["At"], P["qdT"], P["kdec"]
                        pKS = C.ps()
                        mm(pKS, [(H(pKS, h), kTh(h), H(Sb, h), True, True) for h in range(4)], [qkvT, Sb])
                        for h in range(4):
                            stt(DVE, H(r_t, h), H(pKS, h), NEBG[:, h:h + 1], H(vb, h), ALU.mult, ALU.add, [pKS, g, vb], [r_t])
                        yield
                        pV = C.ps()
                        mm(pV, [(H(pV, h), H(TT, h), H(r_t, h), True, True) for h in range(4)], [TT, r_t])
                        cp(ACT, vnew[:], pV[:, :], [pV], [vnew])
                        yield
                        pO = C.ps()
                        specs = []
                        for h in range(4):
                            specs.append((H(pO, h), H(Sb, h), H(qdT, h), True, False))
                            specs.append((H(pO, h), H(vnew, h), H(At, h), False, True))
                        mm(pO, specs, [Sb, qdT, vnew, At])
                        pS = C.ps()
                        mm(pS, [(H(pS, h), H(kdec, h), H(vnew, h), True, True) for h in range(4)], [kdec, vnew])
                        cp(ACT, oT[:, 0:512], pO[:, :], [pO], [oT])
                        for h in range(4):
                            stt(DVE, H(S, h), H(S, h), GL[:, h:h + 1], H(pS, h), ALU.mult, ALU.add, [S, g, pS], [S])
                        cp(ACT, Sb[:], S[:], [S], [Sb])
                        yield

                    def gen_gla(t4, C, oT):
                        cs0 = t4 * 128
                        pL = C.ps()
                        mm(pL, [(pL[:, 0:256], glrT[0:17, cs0:cs0 + 128], w2b[:, :], True, True)], [glrT, w2b])
                        lt = C.p32.get()
                        act(lt[:, 0:256], pL[:, 0:256], AF.Exp, [pL], [lt], scale=-1.0)
                        act(lt[:, 0:256], lt[:, 0:256], AF.Ln, [lt, sm], [lt], bias=ONEC, scale=1.0)
                        yield
                        pB = C.ps()
                        mm(pB, [(pB[:, 0:256], UN16, lt[:, 0:256], True, True)], [cst, lt])
                        pBT = C.ps()
                        mm(pBT, [(pBT[0:64, h * 128:(h + 1) * 128], lt[:, h * 64:(h + 1) * 64], UN16, True, True) for h in range(4)], [cst, lt])
                        ebT = C.p32.get()
                        act(ebT[0:64, :], pBT[0:64, :], AF.Exp, [pBT], [ebT])
                        enbT = C.p32.get()
                        act(enbT[0:64, :], pBT[0:64, :], AF.Exp, [pBT], [enbT], scale=-1.0)
                        enbt = C.p32.get()
                        act(enbt[:, 0:256], pB[:, 0:256], AF.Exp, [pB], [enbt], scale=-1.0)
                        yield
                        qd2 = C.p16.get()
                        kn2 = C.p16.get()
                        for h in range(4):
                            stt(DVE, qd2[0:64, h * 128:(h + 1) * 128], glqkT[:, h * 512 + cs0:h * 512 + cs0 + 128], 0.125, ebT[0:64, h * 128:(h + 1) * 128],
                                ALU.mult, ALU.mult, [glqkT, ebT], [qd2])
                            tt(POOL, kn2[0:64, h * 128:(h + 1) * 128], glqkT[:, (4 + h) * 512 + cs0:(4 + h) * 512 + cs0 + 128], enbT[0:64, h * 128:(h + 1) * 128],
                               ALU.mult, [glqkT, enbT], [kn2])
                        knt = C.p16.get()
                        tt(DVE, knt[:, 0:256], glkt[:, t4 * 256:(t4 + 1) * 256], enbt[:, 0:256], ALU.mult, [glkt, enbt], [knt])
                        yield
                        pA2 = C.ps()
                        mm(pA2, [(H(pA2, h), kn2[0:64, h * 128:(h + 1) * 128], qd2[0:64, h * 128:(h + 1) * 128], True, True) for h in range(4)], [kn2, qd2])
                        At2 = C.p16.get()
                        for h in range(4):
                            tt(DVE, H(At2, h), H(pA2, h), MU, ALU.mult, [pA2, cst], [At2])
                        yield
                        pO2 = C.ps()
                        specs = []
                        for h in range(4):
                            specs.append((H(pO2, h), S2b[:, h * 128:(h + 1) * 128], qd2[0:64, h * 128:(h + 1) * 128], True, False))
                            specs.append((H(pO2, h), glv[:, t4 * 512 + h * 128:t4 * 512 + (h + 1) * 128], H(At2, h), False, True))
                        mm(pO2, specs, [S2b, qd2, glv, At2])
                        pS2 = C.ps()
                        mm(pS2, [(pS2[0:64, h * 128:(h + 1) * 128], knt[:, h * 64:(h + 1) * 64], glv[:, t4 * 512 + h * 128:t4 * 512 + (h + 1) * 128], True, True)
                                 for h in range(4)], [knt, glv])
                        cp(DVE, oT[:, 512:1024], pO2[:, :], [pO2], [oT])
                        tt(DVE, S2[:, :], S2[:, :], pS2[0:64, :], ALU.add, [S2, pS2], [S2])
                        for h in range(4):
                            ts(POOL, S2[:, h * 128:(h + 1) * 128], S2[:, h * 128:(h + 1) * 128], ebT[0:64, h * 128 + 127:h * 128 + 128], ALU.mult, [S2, ebT], [S2])
                        cp(ACT, S2b[:], S2[:], [S2], [S2b])
                        yield

                    def gen_post(t4, C, oT):
                        cs0 = t4 * 128
                        if t4 == 0:
                            dump("oT", oT, oT[:], [128, 1024])
                        for half in range(2):
                            sq_ = C.p16.get()
                            act(sq_[:], oT[:, half * 512:(half + 1) * 512], AF.Square, [oT], [sq_])
                            p = C.ps()
                            mm(p, [(p[:, :], ONESB, sq_[:], True, True)], [cb, sq_])
                            rt = C.p32.get()
                            if os.environ.get("POSTMODE", "0") == "1":
                                act(rt[:], p[:, :], AF.Sqrt, [p, sm], [rt], bias=EPSC, scale=1.0 / 128.0)
                                fw.op(DVE, lambda h: h.reciprocal(out=rt[:], in_=rt[:]), [rt], [rt])
                            else:
                                act(rt[:], p[:, :], AF.Ln, [p, sm], [rt], bias=EPSC, scale=1.0 / 128.0)
                                act(rt[:], rt[:], AF.Exp, [rt], [rt], scale=-0.5)
                            yield
                            tt(DVE, rt[:], rt[:], oT[:, half * 512:(half + 1) * 512], ALU.mult, [rt, oT], [rt])
                            zg3 = zgT[:, half * 2048:(half + 1) * 2048].rearrange("p (h n) -> p h n", h=4)[:, :, cs0:cs0 + 128]
                            og3 = hT[:, half * 2048:(half + 1) * 2048].rearrange("p (h n) -> p h n", h=4)[:, :, cs0:cs0 + 128]
                            tt(DVE, og3, rt[:].rearrange("p (h n) -> p h n", h=4), zg3, ALU.mult, [rt, zgT], [hT])
                            yield

                    def merge(gens):
                        gens = list(gens)
                        while gens:
                            for gq in list(gens):
                                try:
                                    next(gq)
                                except StopIteration:
                                    gens.remove(gq)

                    nb_, ns_ = (b, s + 1) if s + 1 < NS else (b + 1, 0)
                    load_x(nb_, ns_, 0)
                    load_x(nb_, ns_, 1)
                    oTs = {}
                    for rr in range(6):
                        gl_ = []
                        if rr < 4:
                            gl_.append(gen_pre(rr, pset[rr % 2], cpre))
                        if 1 <= rr <= 4:
                            oTs[rr - 1] = oT_pool.get()
                            gl_.append(gen_rec(rr - 1, pset[(rr - 1) % 2], crec, oTs[rr - 1]))
                            gl_.append(gen_gla(rr - 1, cgla, oTs[rr - 1]))
                        if 2 <= rr <= 5:
                            gl_.append(gen_post(rr - 2, cpost, oTs[rr - 2]))
                        merge(gl_)
                        chk(f'round{rr}')

                    dump("ogT", hT, hT[:], [128, 4096], BF16)
                    chk("post")
                    for t4 in range(4):
                        ti = b * NT + s * 4 + t4
                        cs0 = t4 * 128
                        z = z_pool.get()
                        fw.dma(SP, z[:], xr_d[ti], z, [xr_bufs[ti]], [z])
                        for half in range(2):
                            p = ps()
                            mm(p, [(p[:, :], hT[:, c * 512 + cs0:c * 512 + cs0 + 128], wo[:, c * D + half * 512:c * D + (half + 1) * 512], c == 0, c == 7)
                                   for c in range(8)], [hT, wo])
                            tmp = s32.get()
                            tt(DVE, tmp[:], p[:, :], gt1[0][b][:, half * 512:(half + 1) * 512], ALU.mult, [p, gt1[0][b]], [tmp])
                            tt(DVE, z[:, half * 512:(half + 1) * 512], z[:, half * 512:(half + 1) * 512], tmp[:], ALU.add, [z, tmp], [z])
                        st = ln_stats(z, z)
                        act(z[:], z[:], AF.Identity, [z, st], [z], scale=st[:, 16:17], bias=st[:, 17:18])
                        if t4 == 0:
                            dump("x1n", z, z[:], [128, D])
                        x1b = Tl(None, f"x1_{ti}")
                        x1_bufs[ti] = x1b
                        fw.dma(SP, x1_d[ti], z[:], z, [z], [x1b])
            fw.finish()
        chk('phaseA')

        es_b = contextlib.ExitStack()
        with es_b:
            def sbb(shape, dt, name):
                return Tl(es_b.enter_context(nc.sbuf_tensor("sbb_" + name, shape, dt)), name)

            class PoolB:
                def __init__(self, n, shape, dt, nm):
                    self.l = [sbb(shape, dt, f"{nm}{i}") for i in range(n)]
                    self.i = 0

                def get(self):
                    self.i = (self.i + 1) % len(self.l)
                    return self.l[self.i]

            def bcastb(row, name, mul=None):
                t = sbb([128, D], F32, name)
                fw.dma(SP, t[:], vec_d[row, :].partition_broadcast(128), t, [], [t])
                if mul is not None:
                    ts(DVE, t[:], t[:], mul, ALU.mult, [t], [t])
                return t

            g1A = bcastb(2, "g1A", ALPHA)
            b1A = bcastb(3, "b1A", ALPHA)
            g2 = bcastb(4, "g2")
            b2 = bcastb(5, "b2")
            wd = sbb([128, 22 * D], BF16, "wd")
            for q in range(22):
                fw.dma(POOL, wd[:, q * D:(q + 1) * D], wdn_d[q * 128:(q + 1) * 128, :], wd, [], [wd])
            wslb = PoolB(3, [128, 8 * 512], BF16, "wslb")
            wu16_b = [Tl(None, f"wu16_{g}") for g in range(11)]
            x1_pool = PoolB(8, [128, D], F32, "x1t")
            hf_bufs = [sbb([128, 8 * 512], BF16, "hfT0"), sbb([128, 8 * 512], BF16, "hfT1")]
            aT = sbb([128, 22 * 512], BF16, "aT")
            uc_pool = PoolB(4, [128, 514], F32, "uc")
            acc_pool = PoolB(4, [128, 512], F32, "accb")
            carryf = sbb([128, 44 * 2], F32, "carryf")
            tmp_pool = PoolB(3, [128, 512], F32, "tmpb")
            chk('b_load')

            def b_front(b, s, hfT):
                Gb, Bb = GB[(1, b)]
                x1t = []
                for t4 in range(4):
                    ti = b * NT + s * 4 + t4
                    xt = x1_pool.get()
                    x1t.append(xt)
                    fw.dma(SP, xt[:], x1_d[ti], xt, [x1_bufs[ti]], [xt])
                    for half in range(2):
                        p = ps()
                        tr(p, [(p[:, m * 128:(m + 1) * 128], xt[:, (half * 4 + m) * 128:(half * 4 + m + 1) * 128], IDF) for m in range(4)], [xt, cst])
                        for m in range(4):
                            c = half * 4 + m
                            o = hfT[:, c * 512 + t4 * 128: c * 512 + (t4 + 1) * 128]
                            if half == 0:
                                act(o, p[:, m * 128:(m + 1) * 128], AF.Identity, [p, mod], [hfT],
                                    scale=mod[:, Gb + c:Gb + c + 1], bias=mod[:, Bb + c:Bb + c + 1])
                            else:
                                ts(DVE, o, p[:, m * 128:(m + 1) * 128], mod[:, Gb + c:Gb + c + 1], ALU.mult, [p, mod], [hfT],
                                   s2=mod[:, Bb + c:Bb + c + 1], op1=ALU.add)
                    tt(POOL, xt[:], xt[:], g1A[:], ALU.mult, [xt, g1A], [xt])
                    tt(POOL, xt[:], xt[:], b1A[:], ALU.add, [xt, b1A], [xt])
                return x1t

            def b_up(b, s, hfT):
                first = (b == 0 and s == 0)
                if s == 0:
                    fw.op(POOL, lambda h: h.memset(carryf[:], 0.0), [], [carryf])
                for g in range(11):
                    wsl = wslb.get()
                    if first:
                        fw.dma(POOL, wsl[:].rearrange("p (c n) -> p c n", c=8),
                               wup_d.rearrange("(c p) n -> p c n", p=128)[:, :, g * 512:(g + 1) * 512], wsl, [], [wsl])
                        fw.dma(SP, wu16_d[g], wsl[:], wsl, [wsl], [wu16_b[g]])
                    else:
                        fw.dma(SP, wsl[:], wu16_d[g], wsl, [wu16_b[g]], [wsl])
                    for mi in range(4):
                        M = g * 4 + mi
                        p = ps()
                        mm(p, [(p[:, :], wsl[:, c * 512 + mi * 128:c * 512 + (mi + 1) * 128], hfT[:, c * 512:(c + 1) * 512], c == 0, c == 7) for c in range(8)], [wsl, hfT])
                        uc = uc_pool.get()
                        cp(POOL, uc[:, 0:2], carryf[:, M * 2:M * 2 + 2], [carryf], [uc])
                        cp(ACT, uc[:, 2:514], p[:, :], [p], [uc])
                        cp(POOL, carryf[:, M * 2:M * 2 + 2], uc[:, 512:514], [uc], [carryf])
                        acc = acc_pool.get()
                        wc = lambda k: pp[:, O_FC + k * 44 + M:O_FC + k * 44 + M + 1]
                        act(acc[:], uc[:, 2:514], AF.Identity, [uc, pp], [acc], scale=wc(2), bias=pp[:, O_FB + M:O_FB + M + 1])
                        stt(DVE, acc[:], uc[:, 1:513], wc(1), acc[:], ALU.mult, ALU.add, [uc, pp, acc], [acc])
                        stt(DVE, acc[:], uc[:, 0:512], wc(0), acc[:], ALU.mult, ALU.add, [uc, pp, acc], [acc])
                        if M < 22:
                            act(aT[:, M * 512:(M + 1) * 512], acc[:], AF.Silu, [acc], [aT])
                        else:
                            Mg = M - 22
                            tt(DVE, aT[:, Mg * 512:(Mg + 1) * 512], aT[:, Mg * 512:(Mg + 1) * 512], acc[:], ALU.mult, [aT, acc], [aT])

            def b_down(b, s, x1t):
                for t4 in range(4):
                    ti = b * NT + s * 4 + t4
                    cs0 = t4 * 128
                    xt = x1t[t4]
                    for half in range(2):
                        p = ps()
                        mm(p, [(p[:, :], aT[:, c * 512 + cs0:c * 512 + cs0 + 128], wd[:, c * D + half * 512:c * D + (half + 1) * 512], c == 0, c == 21)
                               for c in range(22)], [aT, wd])
                        tmp = tmp_pool.get()
                        tt(DVE, tmp[:], p[:, :], gt1[1][b][:, half * 512:(half + 1) * 512], ALU.mult, [p, gt1[1][b]], [tmp])
                        tt(DVE, xt[:, half * 512:(half + 1) * 512], xt[:, half * 512:(half + 1) * 512], tmp[:], ALU.add, [xt, tmp], [xt])
                    st = ln_stats(xt, xt)
                    act(xt[:], xt[:], AF.Identity, [xt, st], [xt], scale=st[:, 16:17], bias=st[:, 17:18])
                    tt(DVE, xt[:], xt[:], g2[:], ALU.mult, [xt, g2], [xt])
                    tt(POOL, xt[:], xt[:], b2[:], ALU.add, [xt, b2], [xt])
                    ob = Tl(None, f"out{ti}")
                    fw.dma(SP, out_d[b, (s * 4 + t4) * 128:(s * 4 + t4 + 1) * 128, :], xt[:], xt, [xt], [ob])

            seqs = [(b, s) for b in range(NB) for s in range(NS)]
            pend = b_front(seqs[0][0], seqs[0][1], hf_bufs[0])
            for i, (b, s) in enumerate(seqs):
                x1t = pend
                b_up(b, s, hf_bufs[i % 2])
                if i + 1 < len(seqs):
                    pend = b_front(seqs[i + 1][0], seqs[i + 1][1], hf_bufs[(i + 1) % 2])
                b_down(b, s, x1t)
            pade = os.environ.get("PADE", "a")
            for _ in range(int(os.environ.get("PAD", "0"))):
                if "d" in pade:
                    fw.op(DVE, lambda h: h.memset(carryf[:, 0:2], 0.0), [], [carryf])
                if "p" in pade:
                    fw.op(POOL, lambda h: h.memset(carryf[:, 2:4], 0.0), [], [carryf])
                if "a" in pade:
                    act(carryf[:, 4:6], carryf[:, 0:2], AF.Copy, [carryf], [carryf])
            fw.finish()
    except _Stop:
        pass
    return nc, dbg_d


x1_bufs = {}
xr_bufs = {}


def _consts():
    i = np.arange(128)
    k = i[:, None]
    j = i[None, :]
    ident = (k == j).astype(np.float32)
    Uc = (k <= j).astype(np.float32)
    Lp = (k > j).astype(np.float32)
    ones = np.ones((128, 128), np.float32)
    same = (k // 64) == (j // 64)
    M1 = -((k > j) & same).astype(np.float32)
    M2 = ((k >= 64) & (j < 64)).astype(np.float32)
    MU = (k <= j).astype(np.float32)
    UN16 = -Uc / 16.0
    return np.concatenate([ident, Uc, Lp, ones, M1, M2, MU, UN16], axis=1).astype(np.float32)


def _prep(inputs, core, NB):
    f = lambda a: np.ascontiguousarray(np.asarray(a, dtype=np.float32))
    b0 = core * NB
    c = f(inputs["c"])[b0:b0 + NB]
    cT = c.reshape(NB, 8, 128).transpose(2, 1, 0).reshape(128, 8 * NB)
    fm = lambda v: f(v).reshape(-1, 128).T
    pp = np.zeros((128, NP_COLS), np.float32)
    pp[:, O_G0:O_G0 + 8] = fm(inputs["ln0_g"])
    pp[:, O_B0:O_B0 + 8] = fm(inputs["ln0_b"])
    pp[:, O_G1:O_G1 + 8] = fm(inputs["ln1_g"][0])
    pp[:, O_B1:O_B1 + 8] = fm(inputs["ln1_b"][0])
    pp[:, O_BADA:O_BADA + 48] = fm(inputs["b_ada"][0])
    dc = f(inputs["dn_conv"][0])
    for k in range(4):
        pp[:, O_DNC + k * 12:O_DNC + (k + 1) * 12] = fm(dc[k])
    pp[:, O_DNG] = f(inputs["dn_norm_g"][0])
    pp[:, O_GLG] = f(inputs["gla_norm_g"][0])
    fc = f(inputs["ffn_conv"][0])
    for k in range(3):
        pp[:, O_FC + k * 44:O_FC + (k + 1) * 44] = fm(fc[k])
    pp[:, O_FB:O_FB + 44] = fm(inputs["ffn_conv_b"][0])
    pp[:, O_NAL:O_NAL + 4] = f(inputs["dn_a_log"][0])[None, :]
    pp[:, O_DTB:O_DTB + 4] = f(inputs["dn_dt_bias"][0])[None, :]
    ba = f(inputs["b_ada"][0])
    vecs = np.stack([f(inputs["ln0_g"]), f(inputs["ln0_b"]), f(inputs["ln1_g"][0]), f(inputs["ln1_b"][0]),
                     f(inputs["ln2_g"][0]), f(inputs["ln2_b"][0]), ba[2048:3072], ba[5120:6144]], axis=0)
    w2b = np.concatenate([f(inputs["gla_w_gate2"][0]), f(inputs["gla_b_gate"][0])[None, :]], axis=0)
    return {
        "x": f(inputs["x"])[b0:b0 + NB],
        "cT": np.ascontiguousarray(cT),
        "w_ada": f(inputs["w_ada"][0]),
        "w_in": f(inputs["w_in"][0]),
        "w_o": f(inputs["w_o"][0]),
        "w_up": f(inputs["ffn_w_up"][0]),
        "w_down": f(inputs["ffn_w_down"][0]),
        "pp": pp,
        "vecs": np.ascontiguousarray(vecs),
        "w2b": np.ascontiguousarray(w2b),
        "cst": _consts(),
    }


def run(inputs, n_cores=8, dbg=None, stop_at=None):
    x = np.asarray(inputs["x"])
    B, SEQ, _ = x.shape
    NB = B // n_cores
    x1_bufs.clear()
    xr_bufs.clear()
    nc, dbg_d = build(SEQ, NB, dbg, stop_at)
    in_maps = [_prep(inputs, core, NB) for core in range(n_cores)]
    res = run_bass_kernel_spmd(nc, in_maps, core_ids=list(range(n_cores)))
    out = np.concatenate([np.asarray(r["out"]) for r in res.results], axis=0).astype(np.float32)
    return out, res, dbg_d


def kernel(**inputs):
    out, _, _ = run(inputs, 8)
    return out
```
